# Optimizing a Trainium2 kernel written in Bass

```python
import jax, jax.numpy as jnp
from jax import lax
import numpy as np

D_MODEL = 1024
BATCH = 8
SEQ = 2048
DEPTH = 1
DEC_BATCH = 2
DEC_SEQ = 8192
PAST_LEN = 128

N_META = 16
BLOCK = 128
WINDOW = 128
LEAD_PAD = BLOCK - N_META
ROPE_THETA = 10000.0
EPS = 1e-6
NEG = -1e30
MLA_HEADS = 8
MLA_Q_LORA = 384
MLA_KV_LORA = 256
MLA_NOPE = 64
MLA_ROPE = 32
MLA_V = 64
SWA_HEADS = 8
SWA_KV_HEADS = 2
SWA_HEAD_DIM = 64
SWA_GROUP = SWA_HEADS // SWA_KV_HEADS
D_FF = 2816
COL_SIZES = (MLA_Q_LORA, MLA_KV_LORA, MLA_ROPE,
             SWA_HEADS * SWA_HEAD_DIM, SWA_KV_HEADS * SWA_HEAD_DIM, SWA_KV_HEADS * SWA_HEAD_DIM,
             D_MODEL, D_MODEL)
D_IN = sum(COL_SIZES)

kernel_name = "hybrid_mla_window_gqa_macaron_encoder"


def rms_norm(x, g):
    xf = x.astype(jnp.float32)
    y = xf * lax.rsqrt(jnp.mean(xf * xf, axis=-1, keepdims=True) + EPS)
    return (y * g.astype(jnp.float32)).astype(x.dtype)


def rope(x, pos):
    half = x.shape[-1] // 2
    inv = ROPE_THETA ** (-jnp.arange(half, dtype=jnp.float32) / half)
    ang = pos.astype(jnp.float32)[:, None] * inv[None, :]
    cos = jnp.cos(ang)[:, None, :]
    sin = jnp.sin(ang)[:, None, :]
    xf = x.astype(jnp.float32)
    x1, x2 = xf[..., :half], xf[..., half:]
    return jnp.concatenate([x1 * cos - x2 * sin, x2 * cos + x1 * sin], axis=-1).astype(x.dtype)


def swiglu(x, w_in, w_out):
    a = x @ w_in
    g, u = a[..., :D_FF], a[..., D_FF:]
    return (jax.nn.silu(g) * u) @ w_out


def mla(c_q, c_kv, k_rope, pos, key_ok, q_norm_g, w_uq, kv_norm_g, w_ukv):
    B, L, _ = c_q.shape
    q = (rms_norm(c_q, q_norm_g) @ w_uq).reshape(B, L, MLA_HEADS, MLA_NOPE + MLA_ROPE)
    q_nope = q[..., :MLA_NOPE]
    q_rope = rope(q[..., MLA_NOPE:], pos)
    kv = (rms_norm(c_kv, kv_norm_g) @ w_ukv).reshape(B, L, MLA_HEADS, MLA_NOPE + MLA_V)
    k_nope = kv[..., :MLA_NOPE]
    v = kv[..., MLA_NOPE:]
    k_r = rope(k_rope[:, :, None, :], pos)[:, :, 0, :]
    scale = (MLA_NOPE + MLA_ROPE) ** -0.5
    bias = jnp.where(key_ok, 0.0, NEG).astype(jnp.float32)
    nb = L // BLOCK
    qn_b = q_nope.reshape(B, nb, BLOCK, MLA_HEADS, MLA_NOPE).transpose(1, 0, 2, 3, 4)
    qr_b = q_rope.reshape(B, nb, BLOCK, MLA_HEADS, MLA_ROPE).transpose(1, 0, 2, 3, 4)

    def one_block(args):
        qn, qr = args
        s = (jnp.einsum('bqhd,bkhd->bhqk', qn, k_nope).astype(jnp.float32)
             + jnp.einsum('bqhr,bkr->bhqk', qr, k_r).astype(jnp.float32)) * scale + bias
        p = jax.nn.softmax(s, axis=-1)
        return jnp.einsum('bhqk,bkhd->bqhd', p.astype(v.dtype), v)

    o = lax.map(one_block, (qn_b, qr_b))
    return o.transpose(1, 0, 2, 3, 4).reshape(B, L, MLA_HEADS * MLA_V)


def window_gqa(q, k, v, pos, sink):
    B, L, _ = q.shape
    nb = L // BLOCK
    q = rope(q.reshape(B, L, SWA_HEADS, SWA_HEAD_DIM), pos)
    q = q.reshape(B, nb, BLOCK, SWA_KV_HEADS, SWA_GROUP, SWA_HEAD_DIM)
    k = rope(k.reshape(B, L, SWA_KV_HEADS, SWA_HEAD_DIM), pos)
    v = v.reshape(B, L, SWA_KV_HEADS, SWA_HEAD_DIM)
    k_meta = k[:, LEAD_PAD:BLOCK]
    v_meta = v[:, LEAD_PAD:BLOCK]

    def band(t):
        tb = t.reshape(B, nb, BLOCK, SWA_KV_HEADS, SWA_HEAD_DIM)
        tp = jnp.pad(tb, ((0, 0), (1, 1), (0, 0), (0, 0), (0, 0)))
        return jnp.concatenate([tp[:, :-2], tp[:, 1:-1], tp[:, 2:]], axis=2)

    kb, vb = band(k), band(v)
    qi = jnp.arange(L).reshape(nb, BLOCK)
    ki = (jnp.arange(nb)[:, None] - 1) * BLOCK + jnp.arange(3 * BLOCK)[None, :]
    ok = ((ki[:, None, :] >= BLOCK) & (ki[:, None, :] < L)
          & (jnp.abs(qi[:, :, None] - ki[:, None, :]) <= WINDOW))
    scale = SWA_HEAD_DIM ** -0.5
    s_band = jnp.einsum('bnqhgd,bnchd->bnhgqc', q, kb).astype(jnp.float32) * scale
    s_band = jnp.where(ok[None, :, None, None, :, :], s_band, NEG)
    s_meta = jnp.einsum('bnqhgd,bmhd->bnhgqm', q, k_meta).astype(jnp.float32) * scale
    s_sink = jnp.broadcast_to(
        sink.astype(jnp.float32).reshape(SWA_KV_HEADS, SWA_GROUP)[None, None, :, :, None, None],
        s_meta.shape[:-1] + (1,))
    p = jax.nn.softmax(jnp.concatenate([s_band, s_meta, s_sink], axis=-1), axis=-1)
    p_band = p[..., :3 * BLOCK].astype(v.dtype)
    p_meta = p[..., 3 * BLOCK:3 * BLOCK + N_META].astype(v.dtype)
    o = (jnp.einsum('bnhgqc,bnchd->bnqhgd', p_band, vb)
         + jnp.einsum('bnhgqm,bmhd->bnqhgd', p_meta, v_meta))
    return o.reshape(B, L, SWA_HEADS * SWA_HEAD_DIM)


def encoder_layer(x, pos, key_ok,
                  ffn1_pre_g, ffn1_w_in, ffn1_w_out, ffn1_post_g,
                  mix_pre_g, w_in, q_norm_g, w_uq, kv_norm_g, w_ukv, sink,
                  w_o_a, w_o_b, w_out, mix_post_g,
                  ffn2_pre_g, ffn2_w_in, ffn2_w_out, ffn2_post_g):
    x = x + 0.5 * rms_norm(swiglu(rms_norm(x, ffn1_pre_g), ffn1_w_in, ffn1_w_out), ffn1_post_g)
    h = rms_norm(x, mix_pre_g)
    proj = h @ w_in
    splits = np.cumsum(COL_SIZES)[:-1].tolist()
    c_q, c_kv, k_r, q_s, k_s, v_s, g_a, g_b = jnp.split(proj, splits, axis=-1)
    y_a = mla(c_q, c_kv, k_r, pos, key_ok, q_norm_g, w_uq, kv_norm_g, w_ukv) @ w_o_a
    y_b = window_gqa(q_s, k_s, v_s, pos, sink) @ w_o_b
    merged = jax.nn.sigmoid(g_a) * y_a + jax.nn.sigmoid(g_b) * y_b
    x = x + rms_norm(merged @ w_out, mix_post_g)
    x = x + 0.5 * rms_norm(swiglu(rms_norm(x, ffn2_pre_g), ffn2_w_in, ffn2_w_out), ffn2_post_g)
    return x


def run_trunk(x, meta_tokens, weights):
    B, S, D = x.shape
    lead = jnp.concatenate([jnp.zeros((LEAD_PAD, D), x.dtype), meta_tokens.astype(x.dtype)], axis=0)
    h = jnp.concatenate([jnp.broadcast_to(lead[None], (B, BLOCK, D)), x], axis=1)
    L = S + BLOCK
    pos = jnp.arange(L) - LEAD_PAD
    key_ok = pos >= 0
    for l in range(DEPTH):
        h = encoder_layer(h, pos, key_ok, *[w[l] for w in weights])
    return h[:, BLOCK:]


def setup_inputs(seed: int = 0) -> dict:
    key = jax.random.key(seed)
    ks = jax.random.split(key, 24)

    def w(k, shape, fan_in):
        return jax.random.normal(k, shape, jnp.float32) * fan_in ** -0.5

    def gain(k, n):
        return 1.0 + 0.01 * jax.random.normal(k, (DEPTH, n), jnp.float32)

    return {
        "x_prompt": jax.random.normal(ks[0], (BATCH, SEQ, D_MODEL), jnp.float32),
        "x_sample": jax.random.normal(ks[1], (DEC_BATCH, DEC_SEQ, D_MODEL), jnp.float32),
        "meta_tokens": jax.random.normal(ks[2], (N_META, D_MODEL), jnp.float32),
        "ffn1_pre_g": gain(ks[3], D_MODEL),
        "ffn1_w_in": w(ks[4], (DEPTH, D_MODEL, 2 * D_FF), D_MODEL),
        "ffn1_w_out": w(ks[5], (DEPTH, D_FF, D_MODEL), D_FF),
        "ffn1_post_g": gain(ks[6], D_MODEL),
        "mix_pre_g": gain(ks[7], D_MODEL),
        "w_in": w(ks[8], (DEPTH, D_MODEL, D_IN), D_MODEL),
        "q_norm_g": gain(ks[9], MLA_Q_LORA),
        "w_uq": w(ks[10], (DEPTH, MLA_Q_LORA, MLA_HEADS * (MLA_NOPE + MLA_ROPE)), MLA_Q_LORA),
        "kv_norm_g": gain(ks[11], MLA_KV_LORA),
        "w_ukv": w(ks[12], (DEPTH, MLA_KV_LORA, MLA_HEADS * (MLA_NOPE + MLA_V)), MLA_KV_LORA),
        "sink": 0.5 * jax.random.normal(ks[13], (DEPTH, SWA_HEADS), jnp.float32),
        "w_o_a": w(ks[14], (DEPTH, MLA_HEADS * MLA_V, D_MODEL), MLA_HEADS * MLA_V),
        "w_o_b": w(ks[15], (DEPTH, SWA_HEADS * SWA_HEAD_DIM, D_MODEL), SWA_HEADS * SWA_HEAD_DIM),
        "w_out": w(ks[16], (DEPTH, D_MODEL, D_MODEL), D_MODEL),
        "mix_post_g": gain(ks[17], D_MODEL),
        "ffn2_pre_g": gain(ks[18], D_MODEL),
        "ffn2_w_in": w(ks[19], (DEPTH, D_MODEL, 2 * D_FF), D_MODEL),
        "ffn2_w_out": w(ks[20], (DEPTH, D_FF, D_MODEL), D_FF),
        "ffn2_post_g": gain(ks[21], D_MODEL),
    }


def reference(x_prompt, x_sample, meta_tokens,
              ffn1_pre_g, ffn1_w_in, ffn1_w_out, ffn1_post_g,
              mix_pre_g, w_in, q_norm_g, w_uq, kv_norm_g, w_ukv, sink,
              w_o_a, w_o_b, w_out, mix_post_g,
              ffn2_pre_g, ffn2_w_in, ffn2_w_out, ffn2_post_g):
    weights = (ffn1_pre_g, ffn1_w_in, ffn1_w_out, ffn1_post_g,
               mix_pre_g, w_in, q_norm_g, w_uq, kv_norm_g, w_ukv, sink,
               w_o_a, w_o_b, w_out, mix_post_g,
               ffn2_pre_g, ffn2_w_in, ffn2_w_out, ffn2_post_g)
    y_prompt = run_trunk(x_prompt, meta_tokens, weights)
    y_sample = run_trunk(x_sample, meta_tokens, weights)
    return (y_prompt, y_sample)
```

```python
import contextlib

ENGS = ("pe", "act", "dve", "pool", "sp")


class Op:
    __slots__ = ("eng", "fn", "reads", "writes", "dma", "key", "deps", "sig", "sigval", "idx", "inc", "need", "waits")

    def __init__(self, eng, fn, reads, writes, dma, key):
        self.eng = eng
        self.fn = fn
        self.reads = tuple(reads)
        self.writes = tuple(writes)
        self.dma = dma
        self.key = key
        self.deps = ()
        self.sig = False
        self.sigval = 0
        self.need = None


def _base(k):
    return k[0] if isinstance(k, tuple) else k


class Sched:
    def __init__(self, nc):
        self.nc = nc
        self.ops = []
        self.last_write = {}
        self.readers = {}
        self.alias = {}
        self.alias_done = set()
        self.keys_by_base = {}
        self.last_by_key = {}

    def _reg(self, k):
        b = _base(k)
        s = self.keys_by_base.get(b)
        if s is None:
            s = self.keys_by_base[b] = set()
        s.add(k)

    def add(self, eng, fn, reads=(), writes=(), dma=False, key=None, inc=16):
        op = Op(eng, fn, reads, writes, dma, key)
        op.inc = inc
        op.idx = len(self.ops)
        deps = set()
        for r in op.reads:
            w = self.last_write.get(r)
            if w is not None:
                deps.add(w)
        for w_ in op.writes:
            w = self.last_write.get(w_)
            if w is not None:
                deps.add(w)
            rl = self.readers.get(w_)
            if rl:
                deps.update(rl)
            b = _base(w_)
            if b in self.alias and w_ not in self.alias_done:
                self.alias_done.add(w_)
                for ob in self.alias[b]:
                    for k in self.keys_by_base.get(ob, ()):
                        w = self.last_write.get(k)
                        if w is not None:
                            deps.add(w)
                        rl = self.readers.get(k)
                        if rl:
                            deps.update(rl)
        if dma:
            if key is None:
                op.key = op.writes[0]
            prev = self.last_by_key.get(op.key)
            if prev is not None:
                deps.add(prev)
            self.last_by_key[op.key] = op.idx
        deps.discard(op.idx)
        op.deps = deps
        for r in op.reads:
            rl = self.readers.get(r)
            if rl is None:
                self.readers[r] = [op.idx]
                self._reg(r)
            else:
                rl.append(op.idx)
        for w_ in op.writes:
            if w_ not in self.last_write:
                self._reg(w_)
            self.last_write[w_] = op.idx
            self.readers[w_] = []
        self.ops.append(op)
        return op

    def emit(self, final_wait_eng="sp"):
        nc = self.nc
        ops = self.ops
        for op in ops:
            need = {}
            for d in op.deps:
                p = ops[d]
                if p.dma:
                    s = ("d", p.key)
                else:
                    if p.eng == op.eng and p.eng == "pe" and not op.dma:
                        continue
                    s = ("e", p.eng)
                q = need.get(s)
                if q is None or q < d:
                    need[s] = d
            op.need = need
            for d in need.values():
                ops[d].sig = True
        for op in ops:
            if op.dma:
                op.sig = True
        dma_keys = []
        seen = set()
        for op in ops:
            if op.dma and op.key not in seen:
                seen.add(op.key)
                dma_keys.append(op.key)
        self.n_dma_keys = len(dma_keys)
        stack = contextlib.ExitStack()
        with stack:
            esem = {e: stack.enter_context(nc.semaphore(f"s_{e}")) for e in ENGS}
            dsem = {k: stack.enter_context(nc.semaphore(f"d_{i}")) for i, k in enumerate(dma_keys)}
            ecount = {e: 0 for e in ENGS}
            dcount = {k: 0 for k in dma_keys}
            for op in ops:
                if not op.sig:
                    continue
                if op.dma:
                    dcount[op.key] += op.inc
                    op.sigval = dcount[op.key]
                else:
                    ecount[op.eng] += 1
                    op.sigval = ecount[op.eng]
            per_eng = {e: [] for e in ENGS}
            for op in ops:
                per_eng[op.eng].append(op)
            final_waits = [(dsem[k], dcount[k]) for k in dma_keys]
            self.stats = {e: len(per_eng[e]) for e in ENGS}
            self.nwaits = 0

            know = {e: {} for e in ENGS}
            snap = {}
            for op in ops:
                W = know[op.eng]
                pend = []
                for s, d in op.need.items():
                    v = ops[d].sigval
                    if W.get(s, 0) >= v:
                        continue
                    pend.append((s, v, d))
                final = []
                for (s, v, d) in pend:
                    implied = False
                    for (s2, v2, d2) in pend:
                        if d2 == d:
                            continue
                        sn = snap.get(d2)
                        if sn is not None and sn.get(s, 0) >= v:
                            implied = True
                            break
                    if not implied:
                        final.append((s, v))
                for (s, v, d) in pend:
                    if W.get(s, 0) < v:
                        W[s] = v
                    sn = snap.get(d)
                    if sn:
                        for s3, v3 in sn.items():
                            if W.get(s3, 0) < v3:
                                W[s3] = v3
                op.waits = final
                if op.sig:
                    sn = dict(W)
                    own = ("d", op.key) if op.dma else ("e", op.eng)
                    if sn.get(own, 0) < op.sigval:
                        sn[own] = op.sigval
                    snap[op.idx] = sn

            def run(e, eng):
                for op in per_eng[e]:
                    for s, v in op.waits:
                        sem = dsem[s[1]] if s[0] == "d" else esem[s[1]]
                        eng.wait_ge(sem, v)
                        self.nwaits += 1
                    ins = op.fn(eng)
                    if op.sig:
                        if op.dma:
                            ins.then_inc(dsem[op.key], op.inc)
                        else:
                            ins.then_inc(esem[op.eng], 1)
                if e == final_wait_eng:
                    for sem, v in final_waits:
                        if v > 0:
                            eng.wait_ge(sem, v)

            with nc.Block() as block:
                @block.tensor
                def _(eng):
                    run("pe", eng)

                @block.scalar
                def _(eng):
                    run("act", eng)

                @block.vector
                def _(eng):
                    run("dve", eng)

                @block.gpsimd
                def _(eng):
                    run("pool", eng)

                @block.sync
                def _(eng):
                    run("sp", eng)


import math
import contextlib
import numpy as np
import concourse.bass as bass
import concourse.mybir as mybir
from concourse.bass_utils import run_bass_kernel_spmd

F32 = mybir.dt.float32
BF16 = mybir.dt.bfloat16
AF = mybir.ActivationFunctionType
ALU = mybir.AluOpType

NCORES = 8
D = 1024
DFF = 2816
NJ = 22
NCG = 11
NT = 36
NSLOT = NT * 128
NG = 9
QL = 384
KVL = 256
SEQ_P = 2048
SEQ_S = 8192
EPS = 1e-6
SC_MLA = 96 ** -0.5
SC_W = 64 ** -0.5
NEGM = -30000.0
T_META, T_HPREV, T_HNEXT = 32, 33, 34
NVC = 33
NMCG = 9


def _prod(s):
    r = 1
    for v in s:
        r *= v
    return r


class Arena:
    def __init__(self, nc, es, S, nbytes):
        self.cap = nbytes
        self.t = es.enter_context(nc.sbuf_tensor("arena", [128, nbytes // 2], BF16))
        self.S = S
        self.top = 0
        self.scopes = []
        self.live = []
        self.dead = []
        self.peak = 0
        self.kn = {}
        self.n = 0

    def alloc(self, name, free_shape, dt):
        self.n += 1
        lname = name
        name = f"{name}#{self.n}"
        self.kn[lname] = name
        esz = 4 if dt == F32 else 2
        nbytes = _prod(free_shape) * esz
        al = (nbytes + 63) // 64 * 64
        off = self.top
        self.top += al
        self.peak = max(self.peak, self.top)
        assert self.top <= self.cap, f"arena overflow at {name}: {self.top} > {self.cap}"
        ap = self.t[:, off // 2:(off + nbytes) // 2]
        if dt == F32:
            ap = ap.bitcast(F32)
        fs = list(free_shape)
        if len(fs) == 2:
            ap = ap.rearrange("p (a b) -> p a b", a=fs[0])
        elif len(fs) == 3:
            ap = ap.rearrange("p (a b c) -> p a b c", a=fs[0], b=fs[1])
        elif len(fs) == 4:
            ap = ap.rearrange("p (a b c d) -> p a b c d", a=fs[0], b=fs[1], c=fs[2])
        olds = [nm for (nm, o, s) in self.dead if o < off + al and off < o + s]
        if olds:
            self.S.alias[name] = olds
        self.live.append((name, off, al))
        return ap

    def push(self):
        self.scopes.append((self.top, len(self.live)))

    def pop(self):
        top, nl = self.scopes.pop()
        self.dead.extend(self.live[nl:])
        del self.live[nl:]
        self.top = top


def build_program(debug=False, stop_after=None):
    nc = bass.Bass("TRN2", target_bir_lowering=False)
    S = Sched(nc)

    def din(name, shape, dt=F32):
        return nc.dram_tensor(name, list(shape), dt, kind="ExternalInput").ap()

    def dscr(name, shape, dt=BF16):
        if debug:
            return nc.dram_tensor(name, list(shape), dt, kind="ExternalOutput").ap()
        return nc.dram_tensor(name, list(shape), dt).ap()

    xin = din("xin", [NSLOT, D])
    Wf_in = [din("ffn1_w_in", [D, 2 * DFF]), din("ffn2_w_in", [D, 2 * DFF])]
    Wf_out = [din("ffn1_w_out", [DFF, D]), din("ffn2_w_out", [DFF, D])]
    Wmix = din("w_in", [D, 3488])
    Wuq = din("w_uq", [QL, 768])
    Wukv = din("w_ukv", [KVL, 1024])
    Woa = din("w_o_a", [512, D])
    Wob = din("w_o_b", [512, D])
    Wout = din("w_out", [D, D])
    g_all = din("g_all", [8, D])
    sink_d = din("sink", [1, 8])
    tabM = din("tabM", [4, 32, NSLOT])
    tabW = din("tabW", [4, 128, NSLOT])
    onescol_d = din("onescol", [128, NT])
    onesK_d = din("onesK", [128, 2])
    maskPN_d = din("maskPN", [2, 128, 512])
    ident_d = din("ident", [128, 128])
    y = nc.dram_tensor("y", [4096, D], F32, kind="ExternalOutput").ap()

    wbFin = [nc.dram_tensor(f"wbFin{f}", [NCG, 128, 4096], BF16).ap() for f in range(2)]
    wbFout = [nc.dram_tensor(f"wbFout{f}", [128, NJ, D], BF16).ap() for f in range(2)]
    wbMix = nc.dram_tensor("wbMix", [NMCG, 128, 8, 512], BF16).ap()
    x1 = dscr("x1", [4096, D], F32)
    lat_p = dscr("lat_p", [288, 2048])
    lat_m = dscr("lat_m", [288, 512])
    lat_s = [nc.dram_tensor(f"lat_s{i}", [288, 1024], BF16).ap() for i in range(2)]
    ag = [nc.dram_tensor(f"ag{i}", [4 * 288, 1024], BF16).ap() for i in range(2)]
    qA = dscr("qA", [8, 96, 4096])
    qW = dscr("qW", [512, 4096])
    kW = dscr("kW", [128, NSLOT])
    vW = dscr("vW", [NSLOT, 130])
    gT = dscr("gT", [2048, 4096])
    LP, LS = 128 + SEQ_P, 128 + SEQ_S
    NBP, NBS = LP // 128, LS // 128
    Kx = [dscr("Kx_p", [512, LP]), dscr("Kx_s", [512, LS])]
    Vx = [dscr("Vx_p", [8, 128, NBP, 65]), dscr("Vx_s", [8, 128, NBS, 65])]
    oAd = dscr("oAd", [2, 64, 8 * 2048]) if debug else None
    oBd = dscr("oBd", [2, 64, 8 * 2048]) if debug else None
    mgd = dscr("mgd", [2, 128, 8 * 2048]) if debug else None

    es = contextlib.ExitStack()
    with es:
        AR = Arena(nc, es, S, 206 * 1024)
        ps = [es.enter_context(nc.psum_tensor(f"ps{i}", [128, 512], F32)) for i in range(8)]
        psT = ps[7][:].bitcast(BF16).rearrange("p (c n) -> p c n", c=8)

        def k(name, *idx):
            return (AR.kn.get(name, name),) + idx

        def pk(b):
            return (f"ps{b}",)

        _setupn = [0]

        def sdma(eng, out, in_, r, w, slow=False):
            _setupn[0] += 1
            dma(eng, out, in_, r, w, key=("setup", eng, _setupn[0] % 2), slow=slow)

        def dma(eng, out, in_, r, w, key=None, slow=False):
            if key is None:
                w0 = w[0]
                key = (w0[0].split("#")[0],) + tuple(w0[1:])
            if slow:
                S.add(eng, lambda e: e.dma_start(out=out, in_=in_, allow_slow_non_contiguous=True), r, w, dma=True, key=key)
            else:
                S.add(eng, lambda e: e.dma_start(out=out, in_=in_), r, w, dma=True, key=key)

        def mm(out, lhsT, rhs, st, sp, r, w):
            S.add("pe", lambda e: e.matmul(out, lhsT=lhsT, rhs=rhs, start=st, stop=sp), r, w)

        def act(out, in_, func, r, w, **kw):
            S.add("act", lambda e: e.activation(out=out, in_=in_, func=func, **kw), r, w)

        def tt(eng, out, in0, in1, op, r, w):
            S.add(eng, lambda e: e.tensor_tensor(out=out, in0=in0, in1=in1, op=op), r, w)

        def ts(eng, out, in0, s1, op0, r, w):
            S.add(eng, lambda e: e.tensor_scalar(out=out, in0=in0, scalar1=s1, scalar2=None, op0=op0), r, w)

        def stt(eng, out, in0, scalar, in1, op0, op1, r, w):
            S.add(eng, lambda e: e.scalar_tensor_tensor(out=out, in0=in0, scalar=scalar, in1=in1, op0=op0, op1=op1), r, w)

        def cp(eng, out, in_, r, w):
            S.add(eng, lambda e: e.tensor_copy(out=out, in_=in_), r, w)

        def recip(out, in_, r, w):
            S.add("dve", lambda e: e.reciprocal(out=out, in_=in_), r, w)

        def memset(eng, ap, val, w):
            S.add(eng, lambda e: e.memset(ap, val), (), w)

        ident = AR.alloc("ident", [128], BF16)
        ones_bf = AR.alloc("ones_bf", [128], BF16)
        ones64 = AR.alloc("ones64", [64], BF16)
        sel65 = AR.alloc("sel65", [65], BF16)
        cols = AR.alloc("cols", [4], F32)
        graw = AR.alloc("graw", [128], F32)
        identF = AR.alloc("identF", [128], F32)
        gTall = AR.alloc("gTall", [64], F32)
        gq = gTall[:, 48:51]
        gkv = gTall[:, 56:58]
        gpost = AR.alloc("gpost", [3, D], F32)
        maskPN = AR.alloc("maskPN", [2, 512], BF16)
        w_uqP = AR.alloc("w_uqP", [3, 8, 96], BF16)
        w_uqS = AR.alloc("w_uqS", [3, 8, 32], BF16)
        w_uk = AR.alloc("w_uk", [2, 512], BF16)
        w_uv = AR.alloc("w_uv", [2, 512], BF16)
        onescol = AR.alloc("onescol", [NT], F32)
        onesK = AR.alloc("onesK", [2], F32)
        sk = AR.alloc("sk", [8], F32)
        sinkrow = AR.alloc("sinkrow", [2, 512], BF16)
        ssq = AR.alloc("ssq", [16], F32)
        junkA = AR.alloc("junkA", [D], BF16)

        memset("dve", ones_bf, 1.0, [k("ones_bf")])
        memset("dve", ones64, 1.0, [k("ones64")])
        memset("dve", sel65[0:1, :], 0.0, [k("sel65")])
        memset("dve", sel65[0:1, 64:65], 1.0, [k("sel65")])
        memset("dve", cols[:, 0:1], EPS, [k("cols")])
        memset("dve", cols[:, 1:2], 1.0, [k("cols")])
        memset("dve", cols[:, 2:3], math.log(0.5), [k("cols")])
        memset("dve", cols[:, 3:4], 0.0, [k("cols")])
        eps_c, one_c, lnhalf_c = cols[:, 0:1], cols[:, 1:2], cols[:, 2:3]

        sdma("pool", ident, ident_d, [], [k("ident")])
        sdma("pool", maskPN, maskPN_d.rearrange("t p n -> p t n"), [], [k("maskPN")])
        dma("sp", graw[0:64, :], g_all.rearrange("r (c p) -> (r c) p", p=128), [], [k("graw")])
        dma("sp", identF, ident_d, [], [k("identF")])
        S.add("pe", lambda e: e.transpose(out=ps[6][:, 0:64], in_=graw[0:64, :], identity=identF[0:64, 0:64]),
              [k("graw"), k("identF")], [pk(6)])
        cp("dve", gTall, ps[6][:, 0:64], [pk(6)], [k("gTall")])
        def late_setup():
            for i, row in enumerate((1, 3, 5)):
                sdma("sp", gpost[:, i, :], g_all[row:row + 1, :].partition_broadcast(128), [], [k("gpost", i)])
            sdma("sp", onescol, onescol_d, [], [k("onescol")])
            sdma("sp", onesK, onesK_d, [], [k("onesK")])
            sdma("sp", sk[0:1, :], sink_d, [], [k("sk")])
            act(sk[0:1, :], sk[0:1, :], AF.Exp, [k("sk")], [k("sk")])
            for g in range(2):
                cp("dve", sinkrow[0:1, g, :].rearrange("p (j n) -> p j n", j=4),
                   sk[0:1, 4 * g:4 * g + 4].unsqueeze(2).to_broadcast([1, 4, 128]), [k("sk")], [k("sinkrow", g)])

        def cast_ffn_in(f, cg):
            for gu in range(2):
                dst = wbFin[f][cg].rearrange("p (dc gu n) -> p dc gu n", dc=8, gu=2)[:, :, gu, :]
                src = Wf_in[f].rearrange("(dc p) n -> p dc n", p=128)[:, :, gu * DFF + cg * 256: gu * DFF + (cg + 1) * 256]
                dma("pool", dst, src, [], [("wbFin", f, cg, gu)], key=("castk", (cg * 2 + gu) % 4))

        def cast_ffn_out(f):
            for q in range(2):
                dst = wbFout[f][:, q * 11:(q + 1) * 11, :]
                src = Wf_out[f].rearrange("(j p) n -> p j n", p=128)[:, q * 11:(q + 1) * 11, :]
                dma("pool", dst, src, [], [("wbFout", f, q)], key=("castk", q))

        mix_src = Wmix.rearrange("(dc p) n -> p dc n", p=128)
        _castn = [0]

        def cast_mix(v, a, n):
            while n > 0:
                cg, off = v // 512, v % 512
                m = min(n, 512 - off)
                dma("pool", wbMix[cg, :, :, off:off + m], mix_src[:, :, a:a + m], [], [("wbMix", cg, off)],
                    key=("castk", _castn[0] % 4))
                _castn[0] += 1
                v += m
                a += m
                n -= m

        def cast_all_mix():
            cast_mix(384, 384, 256)
            cast_mix(640, 640, 32)
            cast_mix(672, 656, 16)
            cast_mix(688, 640, 16)
            cast_mix(768, 1312, 128)
            cast_mix(896, 1184, 128)
            for g in range(2):
                cast_mix(1024 + g * 64, 1184 + g * 64 + 32, 32)
                cast_mix(1024 + g * 64 + 32, 1184 + g * 64, 32)
            cast_mix(704, 640, 32)
            cast_mix(736, 640, 32)
            cast_mix(0, 0, 384)
            for pr in range(4):
                cast_mix((9 + 2 * pr) * 128, 672 + pr * 128, 128)
                for hh in range(2):
                    vb = (10 + 2 * pr) * 128 + hh * 64
                    sb_ = 672 + pr * 128 + hh * 64
                    cast_mix(vb, sb_ + 32, 32)
                    cast_mix(vb + 32, sb_, 32)
            cast_mix(17 * 128, 1440, 2048)
            cast_mix(33 * 128, 1440, 384)

        uq_src = Wuq.rearrange("(c p) (h d) -> p c h d", p=128, d=96)
        ukv_src = Wukv.rearrange("(c p) (h t d) -> p c h t d", p=128, t=2, d=64)

        def cast_small():
            for c in range(3):
                sdma("pool", w_uqP[:, c, :, 0:32], uq_src[:, c, :, 64:96], [], [k("w_uqP", c, 0)])
                sdma("pool", w_uqP[:, c, :, 32:96], uq_src[:, c, :, 0:64], [], [k("w_uqP", c, 1)])
                sdma("pool", w_uqS[:, c, :, 0:16], uq_src[:, c, :, 80:96], [], [k("w_uqS", c, 0)])
                sdma("pool", w_uqS[:, c, :, 16:32], uq_src[:, c, :, 64:80], [], [k("w_uqS", c, 1)])
            for c in range(2):
                sdma("pool", w_uk[:, c, :].rearrange("p (h d) -> p h d", h=8), ukv_src[:, c, :, 0, :], [], [k("w_uk", c)])
                sdma("pool", w_uv[:, c, :].rearrange("p (h d) -> p h d", h=8), ukv_src[:, c, :, 1, :], [], [k("w_uv", c)])

        PRE = {"done": False}

        def early_casts_a():
            for cg in range(4):
                cast_ffn_in(0, cg)

        def early_casts_b():
            for cg in range(4, NCG):
                cast_ffn_in(0, cg)
            cast_small()
            cast_all_mix()

        _defer_q = []

        def _mk_in(cg, gu):
            def f():
                dst = wbFin[1][cg].rearrange("p (dc gu n) -> p dc gu n", dc=8, gu=2)[:, :, gu, :]
                src = Wf_in[1].rearrange("(dc p) n -> p dc n", p=128)[:, :, gu * DFF + cg * 256: gu * DFF + (cg + 1) * 256]
                dma("pool", dst, src, [], [("wbFin", 1, cg, gu)], key=("castk", (cg * 2 + gu) % 4))
            return f

        def _mk_out(q, hq):
            def f():
                j0 = q * 11 + (0 if hq == 0 else 6)
                j1 = q * 11 + (6 if hq == 0 else 11)
                dst = wbFout[1][:, j0:j1, :]
                src = Wf_out[1].rearrange("(j p) n -> p j n", p=128)[:, j0:j1, :]
                dma("pool", dst, src, [], [("wbFout", 1, q, hq)], key=("castk", (2 * q + hq) % 4))
            return f

        for cg_ in range(NCG):
            for gu_ in range(2):
                _defer_q.append(_mk_in(cg_, gu_))
        for q_ in range(2):
            for hq_ in range(2):
                _defer_q.append(_mk_out(q_, hq_))

        def deferred_casts(g):
            pass

        def trickle_cast(g):
            if g >= 1 and _defer_q:
                _defer_q.pop(0)()

        ring_n = 3
        _ringuse = [0]
        _psrot = [0]
        _ptrot = [0]
        _ssqi = [0]
        _stgi = [0]
        T = {}

        def next_bank(pool=(0, 1, 2, 3)):
            b = pool[_psrot[0] % len(pool)]
            _psrot[0] += 1
            return b

        def rstd_from_ssq(dst, src, n, r, w, half=False):
            np_ = dst.shape[0]
            r = list(r) + [k("cols")]
            act(dst, src, AF.Ln, r, w, scale=1.0 / n, bias=eps_c[0:np_])
            if half:
                act(dst, dst, AF.Exp, list(w) + [k("cols")], w, scale=-0.5, bias=lnhalf_c[0:np_])
            else:
                act(dst, dst, AF.Exp, w, w, scale=-0.5)

        psTb = {7: psT, 6: ps[6][:].bitcast(BF16).rearrange("p (c n) -> p c n", c=8)}

        def norm_front(s, xi=None):
            xi = s if xi is None else xi
            xt_ap = T["xt"][:, xi, :]
            xn = T["xn"][:, s % 2, :]
            i = _ssqi[0] % 8
            _ssqi[0] += 1
            sq_c = ssq[:, i:i + 1]
            rs_c = ssq[:, 8 + i:9 + i]
            act(junkA, xt_ap, AF.Square, [k("xt", xi)], [k("junkA"), k("ssq", i)], accum_out=sq_c)
            rstd_from_ssq(rs_c, sq_c, D, [k("ssq", i)], [k("ssq", 8 + i)])
            ts("dve", xn, xt_ap, rs_c, ALU.mult, [k("xt", xi), k("ssq", 8 + i)], [k("xn", s % 2)])

        def norm_T(s, tb):
            xn = T["xn"][:, s % 2, :]
            for c in range(8):
                S.add("pe", lambda e, c=c: e.transpose(out=psTb[tb][:, c, :], in_=xn[:, c * 128:(c + 1) * 128], identity=ident),
                      [k("xn", s % 2), k("ident")], [pk(tb)])

        def norm_evac(s, tb, gi, dstT, dname):
            tt("dve", dstT[:, :, s * 128:(s + 1) * 128], psTb[tb], gTall[:, 16 * gi:16 * gi + 8].unsqueeze(2).to_broadcast([128, 8, 128]),
               ALU.mult, [pk(tb), k("gTall")], [k(dname, s)])

        def norm_transpose(s, gi, dstT, dname):
            norm_front(s)
            norm_T(s, 7)
            norm_evac(s, 7, gi, dstT, dname)

        def norm_transpose_multi(gi, dstT, dname):
            tbs = (7, 6, 7, 6)
            norm_front(0)
            norm_front(1)
            norm_T(0, tbs[0])
            norm_front(2)
            norm_T(1, tbs[1])
            norm_evac(0, tbs[0], gi, dstT, dname)
            norm_front(3)
            norm_T(2, tbs[2])
            norm_evac(1, tbs[1], gi, dstT, dname)
            norm_T(3, tbs[3])
            norm_evac(2, tbs[2], gi, dstT, dname)
            norm_evac(3, tbs[3], gi, dstT, dname)

        def ffn(f, epilogue, zhook=None):
            xnT, ring, aT, WoutR, tmpE = T["xnT"], T["ring"], T["aT"], T["WoutR"], T["tmpE"]
            for cg in range(NCG):
                u = _ringuse[0]
                _ringuse[0] += 1
                slot = u % ring_n
                dma("sp", ring[:, slot, :], wbFin[f][cg], [("wbFin", f, cg, 0), ("wbFin", f, cg, 1)], [k("ring", slot)])
                wv = ring[:, slot, :].rearrange("p (dc gu n) -> p dc gu n", dc=8, gu=2)
                for jj in range(2):
                    j = 2 * cg + jj
                    bg = (0, 1)[j % 2]
                    bu = (2, 3)[j % 2]
                    for gu, b in ((0, bg), (1, bu)):
                        for dc in range(8):
                            mm(ps[b][:], wv[:, dc, gu, jj * 128:(jj + 1) * 128], xnT[:, dc, :], dc == 0, dc == 7,
                               [k("ring", slot)] + [k("xnT", s_) for s_ in range(4)], [pk(b)])
                    eb = j % 2
                    E = tmpE[:, eb, :]
                    act(E, ps[bg][:], AF.Exp, [pk(bg)], [k("tmpE", eb)], scale=-1.0)
                    act(E, E, AF.Ln, [k("tmpE", eb), k("cols")], [k("tmpE", eb)], bias=one_c)
                    act(E, E, AF.Exp, [k("tmpE", eb)], [k("tmpE", eb)], scale=-1.0)
                    tt("dve", E, E, ps[bg][:], ALU.mult, [k("tmpE", eb), pk(bg)], [k("tmpE", eb)])
                    tt("dve", aT[:, j, :], E, ps[bu][:], ALU.mult, [k("tmpE", eb), pk(bu)], [k("aT", j)])
            pend_e = None
            for s in range(4):
                zb = (4, 5) if s % 2 == 0 else (0, 1)
                for half, b in ((0, zb[0]), (1, zb[1])):
                    for j in range(NJ):
                        mm(ps[b][:], aT[:, j, s * 128:(s + 1) * 128], WoutR[:, j, half * 512:(half + 1) * 512], j == 0, j == NJ - 1,
                           [k("aT", j), k("WoutR", j // 11)], [pk(b)])
                if pend_e is not None:
                    epilogue(*pend_e)
                pend_e = (s, zb)
                if zhook is not None:
                    zhook(s)
            epilogue(*pend_e)

        def post_norm_residual(s, gi, half, zb=(4, 5), xt_ap=None, xkey=None):
            if xt_ap is None:
                xt_ap = T["xt"][:, s, :]
                xkey = k("xt", s)
            tbuf = T["tbuf"]
            i = _ssqi[0] % 8
            _ssqi[0] += 1
            i2 = _ssqi[0] % 8
            _ssqi[0] += 1
            a0, a1 = ssq[:, i:i + 1], ssq[:, i2:i2 + 1]
            rs_c = ssq[:, 8 + i:9 + i]
            act(junkA[:, 0:512], ps[zb[0]][:], AF.Square, [pk(zb[0])], [k("junkA"), k("ssq", i)], accum_out=a0)
            act(junkA[:, 512:1024], ps[zb[1]][:], AF.Square, [pk(zb[1])], [k("junkA"), k("ssq", i2)], accum_out=a1)
            tt("dve", a0, a0, a1, ALU.add, [k("ssq", i), k("ssq", i2)], [k("ssq", i)])
            rstd_from_ssq(rs_c, a0, D, [k("ssq", i)], [k("ssq", 8 + i)], half=half)
            stt("dve", tbuf[:, 0:512], ps[zb[0]][:], rs_c, gpost[:, gi, 0:512], ALU.mult, ALU.mult,
                [pk(zb[0]), k("ssq", 8 + i), k("gpost", gi)], [k("tbuf")])
            stt("dve", tbuf[:, 512:1024], ps[zb[1]][:], rs_c, gpost[:, gi, 512:1024], ALU.mult, ALU.mult,
                [pk(zb[1]), k("ssq", 8 + i), k("gpost", gi)], [k("tbuf")])
            tt("dve", xt_ap, xt_ap, tbuf, ALU.add, [xkey, k("tbuf")], [xkey])

        def alloc_ffn_tiles(tail_mode=False):
            if tail_mode:
                T["xt"] = AR.alloc("xt", [2, D], F32)
                T["xr"] = AR.alloc("xr", [2, D], F32)
            else:
                T["xt"] = AR.alloc("xt", [4, D], F32)
            T["xn"] = AR.alloc("xn", [2, D], BF16)
            T["xnT"] = AR.alloc("xnT", [8, 512], BF16)
            T["aT"] = AR.alloc("aT", [NJ, 512], BF16)
            T["ring"] = AR.alloc("ring", [ring_n, 4096], BF16)
            T["WoutR"] = AR.alloc("WoutR", [NJ, D], BF16)
            T["tmpE"] = AR.alloc("tmpE", [2, 512], F32)
            T["tbuf"] = AR.alloc("tbuf", [D], F32)

        def load_WoutR(f):
            for q in range(2):
                if f == 0:
                    dma("pool", T["WoutR"][:, q * 11:(q + 1) * 11, :],
                        Wf_out[0].rearrange("(j p) n -> p j n", p=128)[:, q * 11:(q + 1) * 11, :], [], [k("WoutR", q)])
                else:
                    dma("sp", T["WoutR"][:, q * 11:(q + 1) * 11, :], wbFout[f][:, q * 11:(q + 1) * 11, :],
                        [("wbFout", f, q, 0), ("wbFout", f, q, 1)], [k("WoutR", q)])

        AR.push()
        alloc_ffn_tiles()
        xt, ring, tmpE = T["xt"], T["ring"], T["tmpE"]
        hT = AR.alloc("hT", [8, 512], BF16)
        cqnT = AR.alloc("cqnT", [3, 512], BF16)
        ckvnT = AR.alloc("ckvnT", [2, 512], BF16)
        sqT = AR.alloc("sqT", [3, 512], BF16)
        rsb = AR.alloc("rsb", [512], F32)
        tM = AR.alloc("tM", [4, 512], F32)
        tW = AR.alloc("tW", [4, 512], F32)
        r1 = AR.alloc("r1", [2, 512], F32)
        r2 = AR.alloc("r2", [2, 512], F32)
        stg = AR.alloc("stg", [4, 512], BF16)
        Vst = AR.alloc("Vst", [4, 2, 65], BF16)
        early_casts_a()
        load_WoutR(0)
        early_casts_b()

        def stage(nparts):
            i = _stgi[0] % 4
            _stgi[0] += 1
            return stg[0:nparts, i, :], k("stg", i)

        def rope_combine(pa, pb, np_, cosT, sinT, tabkey, out, outkey, ra, rb):
            kk_ = _stgi[0] % 2
            tt("dve", r1[0:np_, kk_, :], pa, cosT, ALU.mult, [ra, tabkey], [k("r1", kk_)])
            tt("dve", r2[0:np_, kk_, :], pb, sinT, ALU.mult, [rb, tabkey], [k("r2", kk_)])
            tt("dve", out, r1[0:np_, kk_, :], r2[0:np_, kk_, :], ALU.add, [k("r1", kk_), k("r2", kk_)], [outkey])

        def emit_ag(hf):
            S.add("pool", lambda e: e.collective_compute("AllGather", ALU.bypass, replica_groups=[[0, 1, 2, 3], [4, 5, 6, 7]],
                                                         ins=[lat_s[hf].opt()], outs=[ag[hf].opt()]),
                  [("lat", g_, i_) for g_ in (4 + 2 * hf, 5 + 2 * hf) for i_ in range(2)], [("ag", hf)], dma=True, key=("agk", hf), inc=1)

        G_ORDER = [8, 4, 5, 6, 7, 0, 1, 2, 3]
        for gidx, g in enumerate(G_ORDER):
            sl0 = g * 512
            has_q = g < 8
            if g < 4:
                lat_dst = lat_p[:, g * 512:(g + 1) * 512]
            elif g < 8:
                lat_dst = lat_s[(g - 4) // 2][:, ((g - 4) % 2) * 512:((g - 4) % 2 + 1) * 512]
            else:
                lat_dst = lat_m[:, :]
            def load_x(gg):
                for s in range(4):
                    dma("sp", xt[:, s, :], xin[gg * 512 + s * 128: gg * 512 + (s + 1) * 128, :], [], [k("xt", s)])

            def prenorm():
                norm_transpose_multi(0, T["xnT"], "xnT")

            if gidx == 0:
                load_x(g)
                late_setup()
                prenorm()

            def epi1(s, zb, g=g, sl0=sl0, has_q=has_q):
                if s == 0:
                    dma("sp", tM[0:32], tabM[:, :, sl0:sl0 + 512].rearrange("t p n -> p t n"), [], [k("tM")])
                    dma("sp", tW, tabW[:, :, sl0:sl0 + 512].rearrange("t p n -> p t n"), [], [k("tW")])
                post_norm_residual(s, 0, True, zb)
                if has_q:
                    dma("pool", x1[sl0 + s * 128: sl0 + (s + 1) * 128, :], xt[:, s, :], [k("xt", s)], [("x1", g, s)],
                        key=("x1st", s))
                norm_transpose(s, 1, hT, "hT")

            ffn(0, epi1)

            deferred_casts(g)
            hkeys = [k("hT", s_) for s_ in range(4)]
            loaded = {}

            def need_cg(cgm):
                if cgm in loaded:
                    return loaded[cgm]
                u = _ringuse[0]
                _ringuse[0] += 1
                slot = u % ring_n
                deps = [kk for kk in S.last_write if kk[0] == "wbMix" and kk[1] == cgm]
                dma("sp", ring[:, slot, :], wbMix[cgm].rearrange("p dc n -> p (dc n)"), deps, [k("ring", slot)])
                loaded[cgm] = slot
                return slot

            def proj_chunk(vc, bank, m0=0, m1=128, c0=0, c1=128):
                cgm, ci = vc // 4, vc % 4
                slot = need_cg(cgm)
                wv = ring[:, slot, :].rearrange("p (dc n) -> p dc n", dc=8)
                for dc in range(8):
                    mm(ps[bank][m0:m1, :], wv[:, dc, ci * 128 + c0: ci * 128 + c1], hT[:, dc, :], dc == 0, dc == 7,
                       [k("ring", slot)] + hkeys, [pk(bank)])

            def feat_norm(vcs, banks, n, gcol, gkey, dstT, dname):
                for i, (vc, b) in enumerate(zip(vcs, banks)):
                    proj_chunk(vc, b)
                    act(sqT[:, i, :], ps[b][:], AF.Square, [pk(b)], [k("sqT", i)])
                for i in range(len(vcs)):
                    mm(ps[6][:], ones_bf, sqT[:, i, :], i == 0, i == len(vcs) - 1, [k("ones_bf"), k("sqT", i)], [pk(6)])
                rstd_from_ssq(rsb, ps[6][:], n, [pk(6)], [k("rsb")])
                for i, b in enumerate(banks):
                    stt("dve", dstT[:, i, :], ps[b][:], gcol[:, i:i + 1], rsb, ALU.mult, ALU.mult,
                        [pk(b), k("rsb"), gkey], [k(dname, i)])

            if has_q:
                feat_norm([0, 1, 2], [0, 1, 2], QL, gq, k("gTall"), cqnT, "cqnT")
            feat_norm([3, 4], [3, 0] if has_q else [0, 1], KVL, gkv, k("gTall"), ckvnT, "ckvnT")
            dma("pool", lat_dst[0:256, :].rearrange("(c p) n -> p c n", p=128), ckvnT, [k("ckvnT", 0), k("ckvnT", 1)],
                [("lat", g, 0)], key=("latst", 0))

            ba, bb = 1, 2
            proj_chunk(5, ba, 0, 32, 0, 32)
            proj_chunk(5, bb, 0, 32, 32, 64)
            so, sk_ = stage(32)
            rope_combine(ps[ba][0:32, :], ps[bb][0:32, :], 32, tM[0:32, 2, :], tM[0:32, 3, :], k("tM"), so, sk_, pk(ba), pk(bb))
            dma("pool", lat_dst[256:288, :], so, [sk_], [("lat", g, 1)], key=("latst", 1))

            slot6 = need_cg(1)
            wv6 = ring[:, slot6, :].rearrange("p (dc n) -> p dc n", dc=8)
            psv = ps[3][:].rearrange("p (s n) -> p s n", s=4)
            for s in range(4):
                for dc in range(8):
                    mm(psv[:, s, :], hT[:, dc, s * 128:(s + 1) * 128], wv6[:, dc, 256:384], dc == 0, dc == 7,
                       [k("ring", slot6)] + hkeys, [pk(3)])
            S.add("act", lambda e: e.activation(out=Vst[:, :, :, 0:64], in_=ps[3][:].rearrange("p (s g d) -> p s g d", s=4, g=2),
                                                func=AF.Copy), [pk(3)], [k("Vst")])
            cp("dve", Vst[:, :, :, 64], onescol[:, 4 * g:4 * g + 4].unsqueeze(2).to_broadcast([128, 4, 2]),
               [k("onescol"), k("Vst")], [k("Vst")])
            dma("pool", vW[sl0:sl0 + 512, :].rearrange("(s p) c -> p s c", p=128), Vst.rearrange("p s g e -> p s (g e)"),
                [k("Vst")], [("vW", g)], key=("vWst", 0))

            ba, bb = 0, 1
            proj_chunk(7, ba)
            proj_chunk(8, bb)
            so, sk_ = stage(128)
            rope_combine(ps[ba][:], ps[bb][:], 128, tW[:, 0, :], tW[:, 1, :], k("tW"), so, sk_, pk(ba), pk(bb))
            dma("pool", kW[:, sl0:sl0 + 512], so, [sk_], [("kW", g)], key=("kWst", 0))

            if gidx + 1 < NG and stop_after != ("p1", g):
                load_x(G_ORDER[gidx + 1])
                prenorm()

            if has_q:
                for pr in range(4):
                    ba, bb = ((2, 3), (4, 5), (0, 1))[pr % 3]
                    proj_chunk(9 + 2 * pr, ba)
                    proj_chunk(10 + 2 * pr, bb)
                    so, sk_ = stage(128)
                    rope_combine(ps[ba][:], ps[bb][:], 128, tW[:, 2, :], tW[:, 3, :], k("tW"), so, sk_, pk(ba), pk(bb))
                    dma("pool", qW[pr * 128:(pr + 1) * 128, sl0:sl0 + 512], so, [sk_], [("qW", g, pr)], key=("qWst", pr % 2))
                def gate_chunk(gc):
                    b = (6, 3)[gc % 2]
                    proj_chunk(17 + gc, b)
                    eb = gc % 2
                    E = tmpE[:, eb, :]
                    act(E, ps[b][:], AF.Exp, [pk(b)], [k("tmpE", eb)], scale=-1.0)
                    act(E, E, AF.Ln, [k("tmpE", eb), k("cols")], [k("tmpE", eb)], bias=one_c)
                    so, sk_ = stage(128)
                    act(so, E, AF.Exp, [k("tmpE", eb)], [sk_], scale=-1.0)
                    dma("pool", gT[gc * 128:(gc + 1) * 128, sl0:sl0 + 512], so, [sk_], [("gT", g, gc)], key=("gTst", gc % 2))

                def q_head(h):
                    ba, bb = ((0, 1), (4, 5), (2, 1))[h % 3]
                    for c in range(3):
                        mm(ps[ba][0:96, :], w_uqP[:, c, h, :], cqnT[:, c, :], c == 0, c == 2,
                           [k("w_uqP", c, 0), k("w_uqP", c, 1), k("cqnT", c)], [pk(ba)])
                    for c in range(3):
                        mm(ps[bb][0:32, :], w_uqS[:, c, h, :], cqnT[:, c, :], c == 0, c == 2,
                           [k("w_uqS", c, 0), k("w_uqS", c, 1), k("cqnT", c)], [pk(bb)])
                    so, sk_ = stage(96)
                    act(so[32:64, :], ps[ba][32:64, :], AF.Copy, [pk(ba)], [sk_])
                    act(so[64:96, :], ps[ba][64:96, :], AF.Copy, [pk(ba)], [sk_])
                    rope_combine(ps[ba][0:32, :], ps[bb][0:32, :], 32, tM[0:32, 0, :], tM[0:32, 1, :], k("tM"),
                                 so[0:32, :], sk_, pk(ba), pk(bb))
                    dma("pool", qA[h, :, sl0:sl0 + 512], so, [sk_], [("qA", g, h)], key=("qAst", h % 2))

                for gc in range(16):
                    gate_chunk(gc)
                    trickle_cast(gidx - 1)
                    if gc % 2 == 1:
                        q_head(gc // 2)
            if g == 5:
                emit_ag(0)
            if g == 7:
                emit_ag(1)
            if stop_after == ("p1", g):
                break
        while _defer_q:
            _defer_q.pop(0)()
        AR.pop()

        def finish():
            with nc.allow_low_precision(reason="bf16 matmul operands by design; fp32 accumulation"):
                S.emit()
            nc._sched_stats = (S.stats, S.nwaits, S.n_dma_keys, AR.peak)
            return nc

        if stop_after is not None and stop_after[0] == "p1":
            return finish()

        _epi = {"n": 0, "pend": []}

        def attn_epilogue(bo, out, view4, defer=2, on_dve=False):
            i = _epi["n"] % 2
            _epi["n"] += 1
            recF, recS, numS = T["recF"], T["recS"], T["numS"]
            rF = recF[64:65, i, :]
            if on_dve:
                recip(rF, ps[bo][64:65, :], [pk(bo)], [k("recF", i)])
            else:
                act(rF, ps[bo][64:65, :], AF.Ln, [pk(bo)], [k("recF", i)])
                act(rF, rF, AF.Exp, [k("recF", i)], [k("recF", i)], scale=-1.0)
            cp("dve", recS[64:65, i, 0, :], rF, [k("recF", i)], [k("recS", i, 0)])
            tt("dve", rF, rF, recS[64:65, i, 0, :], ALU.subtract, [k("recF", i), k("recS", i, 0)], [k("recF", i)])
            cp("dve", recS[64:65, i, 1, :], rF, [k("recF", i)], [k("recS", i, 1)])
            if on_dve:
                cp("dve", numS[0:64, i, :], ps[bo][0:64, :], [pk(bo)], [k("numS", i)])
            else:
                act(numS[0:64, i, :], ps[bo][0:64, :], AF.Copy, [pk(bo)], [k("numS", i)])

            def part_b():
                bb_ = (6, 7)[i]
                mm(ps[bb_][0:64, :], ones64[64:65, 0:64], recS[64:65, i, 0, :], True, False, [k("ones64"), k("recS", i, 0)], [pk(bb_)])
                mm(ps[bb_][0:64, :], ones64[64:65, 0:64], recS[64:65, i, 1, :], False, True, [k("ones64"), k("recS", i, 1)], [pk(bb_)])
                if view4:
                    tt("dve", out[0], numS[0:64, i, :].rearrange("p (j n) -> p j n", j=4),
                       ps[bb_][0:64, :].rearrange("p (j n) -> p j n", j=4), ALU.mult, [k("numS", i), pk(bb_)], [out[1]])
                else:
                    tt("dve", out[0], numS[0:64, i, :], ps[bb_][0:64, :], ALU.mult, [k("numS", i), pk(bb_)], [out[1]])

            _epi["pend"].append([defer, part_b])

        def epi_tick(flush=False):
            keep = []
            for item in _epi["pend"]:
                item[0] -= 1
                if flush or item[0] <= 0:
                    item[1]()
                else:
                    keep.append(item)
            _epi["pend"] = keep

        def prep(si):
            AR.push()
            cin = AR.alloc("cin", [2, 2, 512], BF16)
            stgK = AR.alloc("stgK", [2, 4, 512], BF16)
            stgV = AR.alloc("stgV", [2, 8, 4, 65], BF16)
            chunks = []
            if si == 0:
                for i in range(4):
                    chunks.append((lat_p[:, i * 512:(i + 1) * 512], 512, 128 + i * 512, [("lat", i, 0)], 1))
            else:
                for r in range(4):
                    for i in range(4):
                        chunks.append((ag[i // 2][r * 288:(r + 1) * 288, (i % 2) * 512:(i % 2 + 1) * 512], 512,
                                       128 + r * 2048 + i * 512, [("ag", i // 2)], 1))
            chunks.append((lat_m[:, 0:128], 128, 0, [("lat", 8, 0)], 0))
            Kv = Kx[si].rearrange("(pr p) n -> p pr n", p=128)
            for ci, (src, w, k0, rkeys, kcol) in enumerate(chunks):
                buf = ci % 2
                dma("sp", cin[:, buf, :, 0:w], src[0:256, :].rearrange("(c p) n -> p c n", p=128), rkeys, [k("cin", buf)])
                for pr in range(4):
                    b = next_bank((0, 1, 2, 3))
                    for c in range(2):
                        mm(ps[b][:, 0:w], w_uk[:, c, pr * 128:(pr + 1) * 128], cin[:, buf, c, 0:w], c == 0, c == 1,
                           [k("w_uk", c), k("cin", buf)], [pk(b)])
                    act(stgK[:, buf, pr, 0:w], ps[b][:, 0:w], AF.Copy, [pk(b)], [k("stgK", buf, pr)], scale=SC_MLA)
                dma("pool", Kv[:, :, k0:k0 + w], stgK[:, buf, :, 0:w], [k("stgK", buf, pr) for pr in range(4)],
                    [("Kx", si, ci)], key=("Kxst", buf))
                nsb = w // 128
                for s in range(nsb):
                    b = (4, 5)[s % 2]
                    for c in range(2):
                        mm(ps[b][:], cin[:, buf, c, s * 128:(s + 1) * 128], w_uv[:, c, :], c == 0, c == 1,
                           [k("w_uv", c), k("cin", buf)], [pk(b)])
                    cp("dve", stgV[:, buf, :, s, 0:64], ps[b][:].rearrange("p (h d) -> p h d", h=8), [pk(b)], [k("stgV", buf, s)])
                    cp("dve", stgV[:, buf, :, s, 64], onesK[:, kcol:kcol + 1].to_broadcast([128, 8]),
                       [k("onesK"), k("stgV", buf, s)], [k("stgV", buf, s)])
                b0 = k0 // 128
                dma("pool", Vx[si][:, :, b0:b0 + nsb, :].rearrange("h p s e -> p h s e"), stgV[:, buf, :, 0:nsb, :],
                    [k("stgV", buf, s) for s in range(nsb)], [("Vx", si, ci)], key=("Vxst", buf))
            AR.pop()
            return len(chunks)

        def mla(si, nchunks):
            L = LP if si == 0 else LS
            nb = L // 128
            q0 = si * 2048
            oA = T["oA"]
            AR.push()
            KH = AR.alloc("KH", [2, LS], BF16)
            VH = AR.alloc("VH", [2, NBS * 65], BF16)
            QH = AR.alloc("QH", [2, 2048], BF16)
            PT = AR.alloc("PT", [4, 512], BF16)
            T["numS"] = AR.alloc("numS", [2, 512], F32)
            T["recF"] = AR.alloc("recF", [2, 512], F32)
            T["recS"] = AR.alloc("recS", [2, 2, 512], BF16)
            kxkeys = [("Kx", si, ci) for ci in range(nchunks)]
            vxkeys = [("Vx", si, ci) for ci in range(nchunks)]
            LOOK = 3

            def load_head(h):
                kb = h % 2
                dma("sp", KH[32:96, kb, 0:L], Kx[si][h * 64:(h + 1) * 64, 0:L], kxkeys, [k("KH", kb, 0)])
                dma("sp", KH[0:32, kb, 0:128], lat_m[256:288, 0:128], [("lat", 8, 1)], [k("KH", kb, 1)], key=("KHr", kb, 0))
                if si == 0:
                    dma("sp", KH[0:32, kb, 128:L], lat_p[256:288, :], [("lat", g_, 1) for g_ in range(4)], [k("KH", kb, 2)], key=("KHr", kb, 1))
                else:
                    for r in range(4):
                        for hf in range(2):
                            c0_ = 128 + r * 2048 + hf * 1024
                            dma("sp", KH[0:32, kb, c0_: c0_ + 1024], ag[hf][r * 288 + 256:(r + 1) * 288, :],
                                [("ag", hf)], [k("KH", kb, 2 + 2 * r + hf)], key=("KHr", kb, (2 * r + hf) % 4))
                dma("sp", VH[:, kb, 0:nb * 65], Vx[si][h].rearrange("p b e -> p (b e)"), vxkeys, [k("VH", kb)])
                dma("sp", QH[0:96, kb, :], qA[h, :, q0:q0 + 2048], [("qA", g_, h) for g_ in range(si * 4, si * 4 + 4)], [k("QH", kb)])

            nkk = 3 if si == 0 else 10
            units = [(h, qt, b) for h in range(8) for qt in range(4) for b in range(nb)]
            pend = []
            load_head(0)
            for ui, (h, qt, b) in enumerate(units):
                kb = h % 2
                if qt == 0 and b == LOOK + 1 and h + 1 < 8:
                    load_head(h + 1)
                bs = ui % 4
                pt = ui % 4
                kkeys = [k("KH", kb, i_) for i_ in range(nkk)]
                mm(ps[bs][:], KH[0:96, kb, b * 128:(b + 1) * 128], QH[0:96, kb, qt * 512:(qt + 1) * 512], True, True,
                   kkeys + [k("QH", kb)], [pk(bs)])
                act(PT[:, pt, :], ps[bs][:], AF.Exp, [pk(bs)], [k("PT", pt)])
                pend.append((h, qt, b, pt))
                epi_tick()
                if len(pend) > LOOK or ui == len(units) - 1:
                    while pend and (len(pend) > LOOK or ui == len(units) - 1):
                        h2, qt2, b2, pt2 = pend.pop(0)
                        bo = (4, 5)[(h2 * 4 + qt2) % 2]
                        mm(ps[bo][0:65, :], VH[:, h2 % 2, b2 * 65:(b2 + 1) * 65], PT[:, pt2, :], b2 == 0, b2 == nb - 1,
                           [k("VH", h2 % 2), k("PT", pt2)], [pk(bo)])
                        if b2 == nb - 1:
                            attn_epilogue(bo, (oA[0:64, h2, qt2 * 512:(qt2 + 1) * 512], k("oA", h2, qt2)), False,
                                          defer=10, on_dve=True)
            epi_tick(flush=True)
            AR.pop()

        def window(si):
            q0 = si * 2048
            tbase = si * 16
            oB = T["oB"]
            AR.push()
            kWsb = AR.alloc("kWsb", [2, NSLOT], BF16)
            vWsb = AR.alloc("vWsb", [NT, 130], BF16)
            qWsb = AR.alloc("qWsb", [2, 16, 4, 128], BF16)
            PT = AR.alloc("PT", [4, 512], BF16)
            T["numS"] = AR.alloc("numS", [2, 512], F32)
            T["recF"] = AR.alloc("recF", [2, 512], F32)
            T["recS"] = AR.alloc("recS", [2, 2, 512], BF16)
            memset("pool", kWsb[64:128], 0.0, [k("kWsb", "z")])
            memset("pool", qWsb[64:128], 0.0, [k("qWsb", "z")])
            for g in range(2):
                dma("sp", kWsb[0:64, g, :], kW[g * 64:(g + 1) * 64, :], [("kW", g_) for g_ in range(NG)], [k("kWsb", g)])
            dma("sp", vWsb, vW.rearrange("(t p) c -> p t c", p=128), [("vW", g_) for g_ in range(NG)], [k("vWsb")])
            for g in range(2):
                for j in range(4):
                    hh = 4 * g + j
                    dma("sp", qWsb[0:64, g, :, j, :], qW[hh * 64:(hh + 1) * 64, q0:q0 + 2048].rearrange("d (b n) -> d b n", n=128),
                        [("qW", g_, hh // 2) for g_ in range(si * 4, si * 4 + 4)], [k("qWsb", g, j)], key=("qWsbk", j % 2))
            LOOK = 2
            units = []
            for blk in range(16):
                tq = tbase + blk
                prev = tq - 1 if blk > 0 else (None if si == 0 else T_HPREV)
                nxt = tq + 1 if blk < 15 else (None if si == 0 else T_HNEXT)
                kts = []
                if prev is not None:
                    kts.append((prev, 0))
                kts.append((tq, None))
                if nxt is not None:
                    kts.append((nxt, 1))
                kts.append((T_META, None))
                for g in range(2):
                    for i, (kt, mk) in enumerate(kts):
                        units.append((blk, g, i, kt, mk, i == len(kts) - 1))
            pend = []
            for ui, (blk, g, i, kt, mk, last) in enumerate(units):
                bs = ui % 4
                pt = ui % 4
                qap = qWsb[:, g, blk].rearrange("d j n -> d (j n)")
                qkeys = [k("qWsb", g, j) for j in range(4)] + [k("qWsb", "z"), k("kWsb", "z")]
                mm(ps[bs][:], kWsb[:, g, kt * 128:(kt + 1) * 128], qap, True, mk is None, [k("kWsb", g)] + qkeys, [pk(bs)])
                if mk is not None:
                    mm(ps[bs][:], ident, maskPN[:, mk, :], False, True, [k("ident"), k("maskPN")], [pk(bs)])
                act(PT[:, pt, :], ps[bs][:], AF.Exp, [pk(bs)], [k("PT", pt)])
                pend.append((blk, g, i, kt, last, pt))
                epi_tick()
                while pend and (len(pend) > LOOK or ui == len(units) - 1):
                    blk2, g2, i2, kt2, last2, pt2 = pend.pop(0)
                    bo = (4, 5)[(blk2 * 2 + g2) % 2]
                    mm(ps[bo][0:65, :], vWsb[:, kt2, g2 * 65:(g2 + 1) * 65], PT[:, pt2, :], i2 == 0, False,
                       [k("vWsb"), k("PT", pt2)], [pk(bo)])
                    if last2:
                        mm(ps[bo][0:65, :], sel65[0:1, :], sinkrow[0:1, g2, :], False, True, [k("sel65"), k("sinkrow", g2)], [pk(bo)])
                        attn_epilogue(bo, (oB[0:64, 4 * g2:4 * g2 + 4, blk2 * 128:(blk2 + 1) * 128], k("oB", g2, blk2)), True)
            epi_tick(flush=True)
            AR.pop()

        def outproj(si):
            q0 = si * 2048
            oA, oB, mergedT = T["oA"], T["oB"], T["mergedT"]
            AR.push()
            w_oa = AR.alloc("w_oa", [8, D], BF16)
            w_ob = AR.alloc("w_ob", [8, D], BF16)
            gAB = AR.alloc("gAB", [2, 2, 512], BF16)
            t12 = AR.alloc("t12", [2, 2, 512], F32)
            dma("pool", w_oa[0:64], Woa.rearrange("(h p) n -> p h n", p=64), [], [k("w_oa")])
            dma("pool", w_ob[0:64], Wob.rearrange("(h p) n -> p h n", p=64), [], [k("w_ob")])
            memset("pool", w_oa[64:128], 0.0, [k("w_oa", "z")])
            memset("pool", w_ob[64:128], 0.0, [k("w_ob", "z")])
            oAkeys = [k("oA", h, qt) for h in range(8) for qt in range(4)]
            oBkeys = [k("oB", g, blk) for g in range(2) for blk in range(16)]
            for tg in range(4):
                sl = q0 + tg * 512
                gi = si * 4 + tg
                for dmc in range(8):
                    ba, bb = (0, 1) if dmc % 2 == 0 else (2, 3)
                    for h in range(8):
                        mm(ps[ba][:], w_oa[:, h, dmc * 128:(dmc + 1) * 128], oA[:, h, tg * 512:(tg + 1) * 512], h == 0, h == 7,
                           [k("w_oa"), k("w_oa", "z"), k("oA", "z"), k("oA", h, tg)], [pk(ba)])
                    for h in range(8):
                        mm(ps[bb][:], w_ob[:, h, dmc * 128:(dmc + 1) * 128], oB[:, h, tg * 512:(tg + 1) * 512], h == 0, h == 7,
                           [k("w_ob"), k("w_ob", "z"), k("oB", "z")] + [k("oB", h // 4, tg * 4 + b_) for b_ in range(4)], [pk(bb)])
                    gb = dmc % 2
                    dma("sp", gAB[:, 0, gb, :], gT[dmc * 128:(dmc + 1) * 128, sl:sl + 512], [("gT", gi, dmc)], [k("gAB", 0, gb)])
                    dma("sp", gAB[:, 1, gb, :], gT[1024 + dmc * 128:1024 + (dmc + 1) * 128, sl:sl + 512], [("gT", gi, 8 + dmc)],
                        [k("gAB", 1, gb)])
                    tt("dve", t12[:, 0, gb, :], ps[ba][:], gAB[:, 0, gb, :], ALU.mult, [pk(ba), k("gAB", 0, gb)], [k("t12", 0, gb)])
                    tt("dve", t12[:, 1, gb, :], ps[bb][:], gAB[:, 1, gb, :], ALU.mult, [pk(bb), k("gAB", 1, gb)], [k("t12", 1, gb)])
                    tt("dve", mergedT[:, dmc, tg * 512:(tg + 1) * 512], t12[:, 0, gb, :], t12[:, 1, gb, :], ALU.add,
                       [k("t12", 0, gb), k("t12", 1, gb)], [k("mergedT", dmc, tg)])
            if debug:
                dma("pool", oAd[si], oA[0:64].rearrange("p h n -> p (h n)"), oAkeys, [("oAd", si)])
                dma("pool", oBd[si], oB[0:64].rearrange("p h n -> p (h n)"), oBkeys, [("oBd", si)])
                dma("pool", mgd[si], mergedT.rearrange("p c n -> p (c n)"),
                    [k("mergedT", c_, t_) for c_ in range(8) for t_ in range(4)], [("mgd", si)])
            AR.pop()

        def tail(si):
            q0 = si * 2048
            mergedT = T["mergedT"]
            AR.push()
            alloc_ffn_tiles(tail_mode=True)
            xt, xr = T["xt"], T["xr"]
            w_out_sb = AR.alloc("w_out_sb", [8, D], BF16)
            dma("pool", w_out_sb, Wout.rearrange("(c p) n -> p c n", p=128), [], [k("w_out_sb")])
            load_WoutR(1)

            def rows(tg, s):
                sl = q0 + tg * 512 + s * 128
                return sl, sl + 128

            def wout_z(tg, s):
                r0, r1_ = rows(tg, s)
                dma("sp", xt[:, s % 2, :], x1[r0:r1_, :], [("x1", si * 4 + tg, s)], [k("xt", s % 2)])
                for half, b in ((0, 2), (1, 3)):
                    for mc in range(8):
                        mm(ps[b][:], mergedT[:, mc, tg * 512 + s * 128: tg * 512 + (s + 1) * 128],
                           w_out_sb[:, mc, half * 512:(half + 1) * 512], mc == 0, mc == 7,
                           [k("mergedT", mc, tg), k("w_out_sb")], [pk(b)])

            def wout_chain(tg, s):
                r0, r1_ = rows(tg, s)
                post_norm_residual(s, 1, False, (2, 3), xt_ap=xt[:, s % 2, :], xkey=k("xt", s % 2))
                dma("pool", y[r0:r1_, :], xt[:, s % 2, :], [k("xt", s % 2)], [("y", r0)], key=("yst", s % 2))
                norm_front(s, xi=s % 2)

            def wout_T(tg, s):
                norm_T(s, 7)
                norm_evac(s, 7, 2, T["xnT"], "xnT")

            for s in range(4):
                wout_z(0, s)
                wout_chain(0, s)
                wout_T(0, s)
            for tg in range(4):
                def zhook(s, tg=tg):
                    if tg + 1 < 4:
                        wout_z(tg + 1, s)
                        wout_chain(tg + 1, s)
                        if s >= 1:
                            wout_T(tg + 1, s - 1)

                def epi2(s, zb, tg=tg):
                    r0, r1_ = rows(tg, s)
                    dma("sp", xr[:, s % 2, :], y[r0:r1_, :], [("y", r0)], [k("xr", s % 2)])
                    post_norm_residual(s, 2, True, zb, xt_ap=xr[:, s % 2, :], xkey=k("xr", s % 2))
                    dma("pool", y[r0:r1_, :], xr[:, s % 2, :], [k("xr", s % 2)], [("y", r0)], key=("yst2", s % 2))

                ffn(1, epi2, zhook)
                if tg + 1 < 4:
                    wout_T(tg + 1, 3)
            AR.pop()

        for si in range(2):
            nch = prep(si)
            if stop_after == ("prep", si):
                return finish()
            AR.push()
            T["mergedT"] = AR.alloc("mergedT", [8, 2048], BF16)
            AR.push()
            T["oA"] = AR.alloc("oA", [8, 2048], BF16)
            T["oB"] = AR.alloc("oB", [8, 2048], BF16)
            memset("pool", T["oA"][64:128], 0.0, [k("oA", "z")])
            memset("pool", T["oB"][64:128], 0.0, [k("oB", "z")])
            mla(si, nch)
            window(si)
            outproj(si)
            AR.pop()
            if stop_after == ("attn", si):
                return finish()
            tail(si)
            AR.pop()
        return finish()


def _rope_tables(pos):
    def tab(half, rows_rep, scale):
        inv = 10000.0 ** (-np.arange(half, dtype=np.float64) / half)
        ang = pos[None, :] * inv[:, None]
        cos = np.concatenate([np.cos(ang), np.cos(ang)], 0)
        sin = np.concatenate([-np.sin(ang), np.sin(ang)], 0)
        cos = np.concatenate([cos] * rows_rep, 0)
        sin = np.concatenate([sin] * rows_rep, 0)
        return cos, sin
    cM, sM = tab(16, 1, 1.0)
    cW, sW = tab(32, 2, 1.0)
    tabM = np.stack([cM, sM, cM * SC_MLA, sM * SC_MLA]).astype(np.float32)
    tabW = np.stack([cW, sW, cW * SC_W, sW * SC_W]).astype(np.float32)
    return tabM, tabW


def host_prep(inputs):
    f = lambda k: np.asarray(inputs[k], dtype=np.float32)
    xp, xs, meta = f("x_prompt"), f("x_sample"), f("meta_tokens")
    shared = {
        "ffn1_w_in": f("ffn1_w_in")[0], "ffn2_w_in": f("ffn2_w_in")[0],
        "ffn1_w_out": f("ffn1_w_out")[0], "ffn2_w_out": f("ffn2_w_out")[0],
        "w_in": f("w_in")[0], "w_uq": f("w_uq")[0], "w_ukv": f("w_ukv")[0],
        "w_o_a": f("w_o_a")[0], "w_o_b": f("w_o_b")[0], "w_out": f("w_out")[0],
        "sink": f("sink").reshape(1, 8),
    }
    g_all = np.zeros((8, D), np.float32)
    for i, k in enumerate(("ffn1_pre_g", "ffn1_post_g", "mix_pre_g", "mix_post_g", "ffn2_pre_g", "ffn2_post_g")):
        g_all[i] = f(k)[0]
    g_all[6, :QL] = f("q_norm_g")[0]
    g_all[7, :KVL] = f("kv_norm_g")[0]
    shared["g_all"] = g_all
    shared["ident"] = np.eye(128, dtype=np.float32)
    kk = np.arange(128)[:, None]
    qq = np.arange(128)[None, :]
    mP = np.where(kk >= qq, 0.0, NEGM).astype(np.float32)
    mN = np.where(kk <= qq, 0.0, NEGM).astype(np.float32)
    shared["maskPN"] = np.stack([np.tile(mP, (1, 4)), np.tile(mN, (1, 4))]).astype(np.float32)
    onesK = np.ones((128, 2), np.float32)
    onesK[:112, 0] = 0.0
    shared["onesK"] = onesK
    in_maps = []
    for c in range(NCORES):
        sq, ch = c // 4, c % 4
        xin = np.zeros((NSLOT, D), np.float32)
        xin[0:2048] = xp[c]
        xin[2048:4096] = xs[sq, ch * 2048:(ch + 1) * 2048]
        xin[T_META * 128 + 112: T_META * 128 + 128] = meta
        pos = np.zeros(NSLOT, np.float64)
        pos[0:2048] = 16 + np.arange(2048)
        pos[2048:4096] = 16 + ch * 2048 + np.arange(2048)
        pos[T_META * 128:(T_META + 1) * 128] = np.arange(128) - 112
        onescol = np.zeros((128, NT), np.float32)
        onescol[:, 0:32] = 1.0
        onescol[112:, T_META] = 1.0
        if ch > 0:
            xin[T_HPREV * 128:(T_HPREV + 1) * 128] = xs[sq, ch * 2048 - 128: ch * 2048]
            pos[T_HPREV * 128:(T_HPREV + 1) * 128] = 16 + ch * 2048 - 128 + np.arange(128)
            onescol[:, T_HPREV] = 1.0
        if ch < 3:
            xin[T_HNEXT * 128:(T_HNEXT + 1) * 128] = xs[sq, (ch + 1) * 2048: (ch + 1) * 2048 + 128]
            pos[T_HNEXT * 128:(T_HNEXT + 1) * 128] = 16 + (ch + 1) * 2048 + np.arange(128)
            onescol[:, T_HNEXT] = 1.0
        tabM, tabW = _rope_tables(pos)
        m = dict(shared)
        m.update({"xin": xin, "tabM": tabM, "tabW": tabW, "onescol": onescol})
        in_maps.append(m)
    return in_maps


_NC_CACHE = {}


def kernel(**inputs):
    in_maps = host_prep(inputs)
    if "nc" not in _NC_CACHE:
        _NC_CACHE["nc"] = build_program()
    nc = _NC_CACHE["nc"]
    res = run_bass_kernel_spmd(nc, in_maps, core_ids=list(range(NCORES)))
    y_prompt = np.zeros((8, SEQ_P, D), np.float32)
    y_sample = np.zeros((2, SEQ_S, D), np.float32)
    for c in range(NCORES):
        yc = np.asarray(res.results[c]["y"], dtype=np.float32)
        y_prompt[c] = yc[0:2048]
        y_sample[c // 4, (c % 4) * 2048:(c % 4 + 1) * 2048] = yc[2048:4096]
    return (y_prompt, y_sample)
```

```python
import contextlib

ENGS = ("pe", "act", "dve", "pool", "sp")


class Op:
    __slots__ = ("eng", "fn", "reads", "writes", "dma", "key", "deps", "sig", "sigval", "idx", "inc", "need", "waits")

    def __init__(self, eng, fn, reads, writes, dma, key):
        self.eng = eng
        self.fn = fn
        self.reads = tuple(reads)
        self.writes = tuple(writes)
        self.dma = dma
        self.key = key
        self.deps = ()
        self.sig = False
        self.sigval = 0
        self.need = None


def _base(k):
    return k[0] if isinstance(k, tuple) else k


class Sched:
    def __init__(self, nc):
        self.nc = nc
        self.ops = []
        self.last_write = {}
        self.readers = {}
        self.alias = {}
        self.alias_done = set()
        self.keys_by_base = {}
        self.last_by_key = {}

    def _reg(self, k):
        b = _base(k)
        s = self.keys_by_base.get(b)
        if s is None:
            s = self.keys_by_base[b] = set()
        s.add(k)

    def add(self, eng, fn, reads=(), writes=(), dma=False, key=None, inc=16):
        op = Op(eng, fn, reads, writes, dma, key)
        op.inc = inc
        op.idx = len(self.ops)
        deps = set()
        for r in op.reads:
            w = self.last_write.get(r)
            if w is not None:
                deps.add(w)
        for w_ in op.writes:
            w = self.last_write.get(w_)
            if w is not None:
                deps.add(w)
            rl = self.readers.get(w_)
            if rl:
                deps.update(rl)
            b = _base(w_)
            if b in self.alias and w_ not in self.alias_done:
                self.alias_done.add(w_)
                for ob in self.alias[b]:
                    for k in self.keys_by_base.get(ob, ()):
                        w = self.last_write.get(k)
                        if w is not None:
                            deps.add(w)
                        rl = self.readers.get(k)
                        if rl:
                            deps.update(rl)
        if dma:
            if key is None:
                op.key = op.writes[0]
            prev = self.last_by_key.get(op.key)
            if prev is not None:
                deps.add(prev)
            self.last_by_key[op.key] = op.idx
        deps.discard(op.idx)
        op.deps = deps
        for r in op.reads:
            rl = self.readers.get(r)
            if rl is None:
                self.readers[r] = [op.idx]
                self._reg(r)
            else:
                rl.append(op.idx)
        for w_ in op.writes:
            if w_ not in self.last_write:
                self._reg(w_)
            self.last_write[w_] = op.idx
            self.readers[w_] = []
        self.ops.append(op)
        return op

    def emit(self, final_wait_eng="sp"):
        nc = self.nc
        ops = self.ops
        for op in ops:
            need = {}
            for d in op.deps:
                p = ops[d]
                if p.dma:
                    s = ("d", p.key)
                else:
                    if p.eng == op.eng and p.eng == "pe" and not op.dma:
                        continue
                    s = ("e", p.eng)
                q = need.get(s)
                if q is None or q < d:
                    need[s] = d
            op.need = need
            for d in need.values():
                ops[d].sig = True
        for op in ops:
            if op.dma:
                op.sig = True
        dma_keys = []
        seen = set()
        for op in ops:
            if op.dma and op.key not in seen:
                seen.add(op.key)
                dma_keys.append(op.key)
        self.n_dma_keys = len(dma_keys)
        stack = contextlib.ExitStack()
        with stack:
            esem = {e: stack.enter_context(nc.semaphore(f"s_{e}")) for e in ENGS}
            dsem = {k: stack.enter_context(nc.semaphore(f"d_{i}")) for i, k in enumerate(dma_keys)}
            ecount = {e: 0 for e in ENGS}
            dcount = {k: 0 for k in dma_keys}
            for op in ops:
                if not op.sig:
                    continue
                if op.dma:
                    dcount[op.key] += op.inc
                    op.sigval = dcount[op.key]
                else:
                    ecount[op.eng] += 1
                    op.sigval = ecount[op.eng]
            per_eng = {e: [] for e in ENGS}
            for op in ops:
                per_eng[op.eng].append(op)
            final_waits = [(dsem[k], dcount[k]) for k in dma_keys]
            self.stats = {e: len(per_eng[e]) for e in ENGS}
            self.nwaits = 0

            know = {e: {} for e in ENGS}
            snap = {}
            for op in ops:
                W = know[op.eng]
                pend = []
                for s, d in op.need.items():
                    v = ops[d].sigval
                    if W.get(s, 0) >= v:
                        continue
                    pend.append((s, v, d))
                final = []
                for (s, v, d) in pend:
                    implied = False
                    for (s2, v2, d2) in pend:
                        if d2 == d:
                            continue
                        sn = snap.get(d2)
                        if sn is not None and sn.get(s, 0) >= v:
                            implied = True
                            break
                    if not implied:
                        final.append((s, v))
                for (s, v, d) in pend:
                    if W.get(s, 0) < v:
                        W[s] = v
                    sn = snap.get(d)
                    if sn:
                        for s3, v3 in sn.items():
                            if W.get(s3, 0) < v3:
                                W[s3] = v3
                op.waits = final
                if op.sig:
                    sn = dict(W)
                    own = ("d", op.key) if op.dma else ("e", op.eng)
                    if sn.get(own, 0) < op.sigval:
                        sn[own] = op.sigval
                    snap[op.idx] = sn

            def run(e, eng):
                for op in per_eng[e]:
                    for s, v in op.waits:
                        sem = dsem[s[1]] if s[0] == "d" else esem[s[1]]
                        eng.wait_ge(sem, v)
                        self.nwaits += 1
                    ins = op.fn(eng)
                    if op.sig:
                        if op.dma:
                            ins.then_inc(dsem[op.key], op.inc)
                        else:
                            ins.then_inc(esem[op.eng], 1)
                if e == final_wait_eng:
                    for sem, v in final_waits:
                        if v > 0:
                            eng.wait_ge(sem, v)

            with nc.Block() as block:
                @block.tensor
                def _(eng):
                    run("pe", eng)

                @block.scalar
                def _(eng):
                    run("act", eng)

                @block.vector
                def _(eng):
                    run("dve", eng)

                @block.gpsimd
                def _(eng):
                    run("pool", eng)

                @block.sync
                def _(eng):
                    run("sp", eng)


import math
import contextlib
import numpy as np
import concourse.bass as bass
import concourse.mybir as mybir
from concourse.bass_utils import run_bass_kernel_spmd

F32 = mybir.dt.float32
BF16 = mybir.dt.bfloat16
AF = mybir.ActivationFunctionType
ALU = mybir.AluOpType

NCORES = 8
D = 1024
DFF = 2816
NJ = 22
NCG = 11
NT = 36
NSLOT = NT * 128
NG = 9
QL = 384
KVL = 256
SEQ_P = 2048
SEQ_S = 8192
EPS = 1e-6
SC_MLA = 96 ** -0.5
SC_W = 64 ** -0.5
NEGM = -30000.0
T_META, T_HPREV, T_HNEXT = 32, 33, 34
NVC = 33
NMCG = 9


def _prod(s):
    r = 1
    for v in s:
        r *= v
    return r


class Arena:
    def __init__(self, nc, es, S, nbytes):
        self.cap = nbytes
        self.t = es.enter_context(nc.sbuf_tensor("arena", [128, nbytes // 2], BF16))
        self.S = S
        self.top = 0
        self.scopes = []
        self.live = []
        self.dead = []
        self.peak = 0
        self.kn = {}
        self.n = 0

    def alloc(self, name, free_shape, dt):
        self.n += 1
        lname = name
        name = f"{name}#{self.n}"
        self.kn[lname] = name
        esz = 4 if dt == F32 else 2
        nbytes = _prod(free_shape) * esz
        al = (nbytes + 63) // 64 * 64
        off = self.top
        self.top += al
        self.peak = max(self.peak, self.top)
        assert self.top <= self.cap, f"arena overflow at {name}: {self.top} > {self.cap}"
        ap = self.t[:, off // 2:(off + nbytes) // 2]
        if dt == F32:
            ap = ap.bitcast(F32)
        fs = list(free_shape)
        if len(fs) == 2:
            ap = ap.rearrange("p (a b) -> p a b", a=fs[0])
        elif len(fs) == 3:
            ap = ap.rearrange("p (a b c) -> p a b c", a=fs[0], b=fs[1])
        elif len(fs) == 4:
            ap = ap.rearrange("p (a b c d) -> p a b c d", a=fs[0], b=fs[1], c=fs[2])
        olds = [nm for (nm, o, s) in self.dead if o < off + al and off < o + s]
        if olds:
            self.S.alias[name] = olds
        self.live.append((name, off, al))
        return ap

    def push(self):
        self.scopes.append((self.top, len(self.live)))

    def pop(self):
        top, nl = self.scopes.pop()
        self.dead.extend(self.live[nl:])
        del self.live[nl:]
        self.top = top


def build_program(debug=False, stop_after=None):
    nc = bass.Bass("TRN2", target_bir_lowering=False)
    S = Sched(nc)

    def din(name, shape, dt=F32):
        return nc.dram_tensor(name, list(shape), dt, kind="ExternalInput").ap()

    def dscr(name, shape, dt=BF16):
        if debug:
            return nc.dram_tensor(name, list(shape), dt, kind="ExternalOutput").ap()
        return nc.dram_tensor(name, list(shape), dt).ap()

    xin = din("xin", [NSLOT, D])
    Wf_in = [din("ffn1_w_in", [D, 2 * DFF]), din("ffn2_w_in", [D, 2 * DFF])]
    Wf_out = [din("ffn1_w_out", [DFF, D]), din("ffn2_w_out", [DFF, D])]
    Wmix = din("w_in", [D, 3488])
    Wuq = din("w_uq", [QL, 768])
    Wukv = din("w_ukv", [KVL, 1024])
    Woa = din("w_o_a", [512, D])
    Wob = din("w_o_b", [512, D])
    Wout = din("w_out", [D, D])
    g_all = din("g_all", [8, D])
    sink_d = din("sink", [1, 8])
    tabM = din("tabM", [4, 32, NSLOT])
    tabW = din("tabW", [4, 128, NSLOT])
    onescol_d = din("onescol", [128, NT])
    onesK_d = din("onesK", [128, 2])
    maskPN_d = din("maskPN", [2, 128, 512])
    ident_d = din("ident", [128, 128])
    y = nc.dram_tensor("y", [4096, D], F32, kind="ExternalOutput").ap()

    wbFin = [nc.dram_tensor(f"wbFin{f}", [NCG, 128, 4096], BF16).ap() for f in range(2)]
    wbFout = [nc.dram_tensor(f"wbFout{f}", [128, NJ, D], BF16).ap() for f in range(2)]
    wbMix = nc.dram_tensor("wbMix", [NMCG, 128, 8, 512], BF16).ap()
    x1 = dscr("x1", [4096, D], F32)
    lat_p = dscr("lat_p", [288, 2048])
    lat_m = dscr("lat_m", [288, 512])
    lat_s = [nc.dram_tensor(f"lat_s{i}", [288, 1024], BF16).ap() for i in range(2)]
    ag = [nc.dram_tensor(f"ag{i}", [4 * 288, 1024], BF16).ap() for i in range(2)]
    qA = dscr("qA", [8, 96, 4096])
    qW = dscr("qW", [512, 4096])
    kW = dscr("kW", [128, NSLOT])
    vW = dscr("vW", [NSLOT, 130])
    gT = dscr("gT", [2048, 4096])
    LP, LS = 128 + SEQ_P, 128 + SEQ_S
    NBP, NBS = LP // 128, LS // 128
    Kx = [dscr("Kx_p", [512, LP]), dscr("Kx_s", [512, LS])]
    Vx = [dscr("Vx_p", [8, 128, NBP, 65]), dscr("Vx_s", [8, 128, NBS, 65])]
    oAd = dscr("oAd", [2, 64, 8 * 2048]) if debug else None
    oBd = dscr("oBd", [2, 64, 8 * 2048]) if debug else None
    mgd = dscr("mgd", [2, 128, 8 * 2048]) if debug else None

    es = contextlib.ExitStack()
    with es:
        AR = Arena(nc, es, S, 206 * 1024)
        ps = [es.enter_context(nc.psum_tensor(f"ps{i}", [128, 512], F32)) for i in range(8)]
        psT = ps[7][:].bitcast(BF16).rearrange("p (c n) -> p c n", c=8)

        def k(name, *idx):
            return (AR.kn.get(name, name),) + idx

        def pk(b):
            return (f"ps{b}",)

        _setupn = [0]

        def sdma(eng, out, in_, r, w, slow=False):
            _setupn[0] += 1
            dma(eng, out, in_, r, w, key=("setup", eng, _setupn[0] % 2), slow=slow)

        def dma(eng, out, in_, r, w, key=None, slow=False):
            if key is None:
                w0 = w[0]
                key = (w0[0].split("#")[0],) + tuple(w0[1:])
            if slow:
                S.add(eng, lambda e: e.dma_start(out=out, in_=in_, allow_slow_non_contiguous=True), r, w, dma=True, key=key)
            else:
                S.add(eng, lambda e: e.dma_start(out=out, in_=in_), r, w, dma=True, key=key)

        def mm(out, lhsT, rhs, st, sp, r, w):
            S.add("pe", lambda e: e.matmul(out, lhsT=lhsT, rhs=rhs, start=st, stop=sp), r, w)

        def act(out, in_, func, r, w, **kw):
            S.add("act", lambda e: e.activation(out=out, in_=in_, func=func, **kw), r, w)

        def tt(eng, out, in0, in1, op, r, w):
            S.add(eng, lambda e: e.tensor_tensor(out=out, in0=in0, in1=in1, op=op), r, w)

        def ts(eng, out, in0, s1, op0, r, w):
            S.add(eng, lambda e: e.tensor_scalar(out=out, in0=in0, scalar1=s1, scalar2=None, op0=op0), r, w)

        def stt(eng, out, in0, scalar, in1, op0, op1, r, w):
            S.add(eng, lambda e: e.scalar_tensor_tensor(out=out, in0=in0, scalar=scalar, in1=in1, op0=op0, op1=op1), r, w)

        def cp(eng, out, in_, r, w):
            S.add(eng, lambda e: e.tensor_copy(out=out, in_=in_), r, w)

        def recip(out, in_, r, w):
            S.add("dve", lambda e: e.reciprocal(out=out, in_=in_), r, w)

        def memset(eng, ap, val, w):
            S.add(eng, lambda e: e.memset(ap, val), (), w)

        ident = AR.alloc("ident", [128], BF16)
        ones_bf = AR.alloc("ones_bf", [128], BF16)
        ones64 = AR.alloc("ones64", [64], BF16)
        sel65 = AR.alloc("sel65", [65], BF16)
        cols = AR.alloc("cols", [4], F32)
        graw = AR.alloc("graw", [128], F32)
        identF = AR.alloc("identF", [128], F32)
        gTall = AR.alloc("gTall", [64], F32)
        gq = gTall[:, 48:51]
        gkv = gTall[:, 56:58]
        gpost = AR.alloc("gpost", [3, D], F32)
        maskPN = AR.alloc("maskPN", [2, 512], BF16)
        w_uqP = AR.alloc("w_uqP", [3, 8, 96], BF16)
        w_uqS = AR.alloc("w_uqS", [3, 8, 32], BF16)
        w_uk = AR.alloc("w_uk", [2, 512], BF16)
        w_uv = AR.alloc("w_uv", [2, 512], BF16)
        onescol = AR.alloc("onescol", [NT], F32)
        onesK = AR.alloc("onesK", [2], F32)
        sk = AR.alloc("sk", [8], F32)
        sinkrow = AR.alloc("sinkrow", [2, 512], BF16)
        ssq = AR.alloc("ssq", [16], F32)
        junkA = AR.alloc("junkA", [D], BF16)

        memset("dve", ones_bf, 1.0, [k("ones_bf")])
        memset("dve", ones64, 1.0, [k("ones64")])
        memset("dve", sel65[0:1, :], 0.0, [k("sel65")])
        memset("dve", sel65[0:1, 64:65], 1.0, [k("sel65")])
        memset("dve", cols[:, 0:1], EPS, [k("cols")])
        memset("dve", cols[:, 1:2], 1.0, [k("cols")])
        memset("dve", cols[:, 2:3], math.log(0.5), [k("cols")])
        memset("dve", cols[:, 3:4], 0.0, [k("cols")])
        eps_c, one_c, lnhalf_c = cols[:, 0:1], cols[:, 1:2], cols[:, 2:3]

        sdma("pool", ident, ident_d, [], [k("ident")])
        sdma("pool", maskPN, maskPN_d.rearrange("t p n -> p t n"), [], [k("maskPN")])
        dma("sp", graw[0:64, :], g_all.rearrange("r (c p) -> (r c) p", p=128), [], [k("graw")])
        dma("sp", identF, ident_d, [], [k("identF")])
        S.add("pe", lambda e: e.transpose(out=ps[6][:, 0:64], in_=graw[0:64, :], identity=identF[0:64, 0:64]),
              [k("graw"), k("identF")], [pk(6)])
        cp("dve", gTall, ps[6][:, 0:64], [pk(6)], [k("gTall")])
        def late_setup():
            for i, row in enumerate((1, 3, 5)):
                sdma("sp", gpost[:, i, :], g_all[row:row + 1, :].partition_broadcast(128), [], [k("gpost", i)])
            sdma("sp", onescol, onescol_d, [], [k("onescol")])
            sdma("sp", onesK, onesK_d, [], [k("onesK")])
            sdma("sp", sk[0:1, :], sink_d, [], [k("sk")])
            act(sk[0:1, :], sk[0:1, :], AF.Exp, [k("sk")], [k("sk")])
            for g in range(2):
                cp("dve", sinkrow[0:1, g, :].rearrange("p (j n) -> p j n", j=4),
                   sk[0:1, 4 * g:4 * g + 4].unsqueeze(2).to_broadcast([1, 4, 128]), [k("sk")], [k("sinkrow", g)])

        def cast_ffn_in(f, cg):
            for gu in range(2):
                dst = wbFin[f][cg].rearrange("p (dc gu n) -> p dc gu n", dc=8, gu=2)[:, :, gu, :]
                src = Wf_in[f].rearrange("(dc p) n -> p dc n", p=128)[:, :, gu * DFF + cg * 256: gu * DFF + (cg + 1) * 256]
                dma("pool", dst, src, [], [("wbFin", f, cg, gu)], key=("castk", (cg * 2 + gu) % 4))

        def cast_ffn_out(f):
            for q in range(2):
                dst = wbFout[f][:, q * 11:(q + 1) * 11, :]
                src = Wf_out[f].rearrange("(j p) n -> p j n", p=128)[:, q * 11:(q + 1) * 11, :]
                dma("pool", dst, src, [], [("wbFout", f, q)], key=("castk", q))

        mix_src = Wmix.rearrange("(dc p) n -> p dc n", p=128)
        _castn = [0]

        def cast_mix(v, a, n):
            while n > 0:
                cg, off = v // 512, v % 512
                m = min(n, 512 - off)
                dma("pool", wbMix[cg, :, :, off:off + m], mix_src[:, :, a:a + m], [], [("wbMix", cg, off)],
                    key=("castk", _castn[0] % 4))
                _castn[0] += 1
                v += m
                a += m
                n -= m

        def cast_all_mix():
            cast_mix(384, 384, 256)
            cast_mix(640, 640, 32)
            cast_mix(672, 656, 16)
            cast_mix(688, 640, 16)
            cast_mix(768, 1312, 128)
            cast_mix(896, 1184, 128)
            for g in range(2):
                cast_mix(1024 + g * 64, 1184 + g * 64 + 32, 32)
                cast_mix(1024 + g * 64 + 32, 1184 + g * 64, 32)
            cast_mix(704, 640, 32)
            cast_mix(736, 640, 32)
            cast_mix(0, 0, 384)
            for pr in range(4):
                cast_mix((9 + 2 * pr) * 128, 672 + pr * 128, 128)
                for hh in range(2):
                    vb = (10 + 2 * pr) * 128 + hh * 64
                    sb_ = 672 + pr * 128 + hh * 64
                    cast_mix(vb, sb_ + 32, 32)
                    cast_mix(vb + 32, sb_, 32)
            cast_mix(17 * 128, 1440, 2048)
            cast_mix(33 * 128, 1440, 384)

        uq_src = Wuq.rearrange("(c p) (h d) -> p c h d", p=128, d=96)
        ukv_src = Wukv.rearrange("(c p) (h t d) -> p c h t d", p=128, t=2, d=64)

        def cast_small():
            for c in range(3):
                sdma("pool", w_uqP[:, c, :, 0:32], uq_src[:, c, :, 64:96], [], [k("w_uqP", c, 0)])
                sdma("pool", w_uqP[:, c, :, 32:96], uq_src[:, c, :, 0:64], [], [k("w_uqP", c, 1)])
                sdma("pool", w_uqS[:, c, :, 0:16], uq_src[:, c, :, 80:96], [], [k("w_uqS", c, 0)])
                sdma("pool", w_uqS[:, c, :, 16:32], uq_src[:, c, :, 64:80], [], [k("w_uqS", c, 1)])
            for c in range(2):
                sdma("pool", w_uk[:, c, :].rearrange("p (h d) -> p h d", h=8), ukv_src[:, c, :, 0, :], [], [k("w_uk", c)])
                sdma("pool", w_uv[:, c, :].rearrange("p (h d) -> p h d", h=8), ukv_src[:, c, :, 1, :], [], [k("w_uv", c)])

        PRE = {"done": False}

        def early_casts_a():
            for cg in range(4):
                cast_ffn_in(0, cg)

        def early_casts_b():
            for cg in range(4, NCG):
                cast_ffn_in(0, cg)
            cast_small()
            cast_all_mix()

        _defer_q = []

        def _mk_in(cg, gu):
            def f():
                dst = wbFin[1][cg].rearrange("p (dc gu n) -> p dc gu n", dc=8, gu=2)[:, :, gu, :]
                src = Wf_in[1].rearrange("(dc p) n -> p dc n", p=128)[:, :, gu * DFF + cg * 256: gu * DFF + (cg + 1) * 256]
                dma("pool", dst, src, [], [("wbFin", 1, cg, gu)], key=("castk", (cg * 2 + gu) % 4))
            return f

        def _mk_out(q, hq):
            def f():
                j0 = q * 11 + (0 if hq == 0 else 6)
                j1 = q * 11 + (6 if hq == 0 else 11)
                dst = wbFout[1][:, j0:j1, :]
                src = Wf_out[1].rearrange("(j p) n -> p j n", p=128)[:, j0:j1, :]
                dma("pool", dst, src, [], [("wbFout", 1, q, hq)], key=("castk", (2 * q + hq) % 4))
            return f

        for cg_ in range(NCG):
            for gu_ in range(2):
                _defer_q.append(_mk_in(cg_, gu_))
        for q_ in range(2):
            for hq_ in range(2):
                _defer_q.append(_mk_out(q_, hq_))

        def deferred_casts(g):
            pass

        def trickle_cast(g):
            if g >= 1 and _defer_q:
                _defer_q.pop(0)()

        ring_n = 3
        _ringuse = [0]
        _psrot = [0]
        _ptrot = [0]
        _ssqi = [0]
        _stgi = [0]
        T = {}

        def next_bank(pool=(0, 1, 2, 3)):
            b = pool[_psrot[0] % len(pool)]
            _psrot[0] += 1
            return b

        def rstd_from_ssq(dst, src, n, r, w, half=False):
            np_ = dst.shape[0]
            r = list(r) + [k("cols")]
            act(dst, src, AF.Ln, r, w, scale=1.0 / n, bias=eps_c[0:np_])
            if half:
                act(dst, dst, AF.Exp, list(w) + [k("cols")], w, scale=-0.5, bias=lnhalf_c[0:np_])
            else:
                act(dst, dst, AF.Exp, w, w, scale=-0.5)

        psTb = {7: psT, 6: ps[6][:].bitcast(BF16).rearrange("p (c n) -> p c n", c=8)}

        def norm_front(s, xi=None):
            xi = s if xi is None else xi
            xt_ap = T["xt"][:, xi, :]
            xn = T["xn"][:, s % 2, :]
            i = _ssqi[0] % 8
            _ssqi[0] += 1
            sq_c = ssq[:, i:i + 1]
            rs_c = ssq[:, 8 + i:9 + i]
            act(junkA, xt_ap, AF.Square, [k("xt", xi)], [k("junkA"), k("ssq", i)], accum_out=sq_c)
            rstd_from_ssq(rs_c, sq_c, D, [k("ssq", i)], [k("ssq", 8 + i)])
            ts("dve", xn, xt_ap, rs_c, ALU.mult, [k("xt", xi), k("ssq", 8 + i)], [k("xn", s % 2)])

        def norm_T(s, tb):
            xn = T["xn"][:, s % 2, :]
            for c in range(8):
                S.add("pe", lambda e, c=c: e.transpose(out=psTb[tb][:, c, :], in_=xn[:, c * 128:(c + 1) * 128], identity=ident),
                      [k("xn", s % 2), k("ident")], [pk(tb)])

        def norm_evac(s, tb, gi, dstT, dname):
            tt("dve", dstT[:, :, s * 128:(s + 1) * 128], psTb[tb], gTall[:, 16 * gi:16 * gi + 8].unsqueeze(2).to_broadcast([128, 8, 128]),
               ALU.mult, [pk(tb), k("gTall")], [k(dname, s)])

        def norm_transpose(s, gi, dstT, dname):
            norm_front(s)
            norm_T(s, 7)
            norm_evac(s, 7, gi, dstT, dname)

        def norm_transpose_multi(gi, dstT, dname):
            tbs = (7, 6, 7, 6)
            norm_front(0)
            norm_front(1)
            norm_T(0, tbs[0])
            norm_front(2)
            norm_T(1, tbs[1])
            norm_evac(0, tbs[0], gi, dstT, dname)
            norm_front(3)
            norm_T(2, tbs[2])
            norm_evac(1, tbs[1], gi, dstT, dname)
            norm_T(3, tbs[3])
            norm_evac(2, tbs[2], gi, dstT, dname)
            norm_evac(3, tbs[3], gi, dstT, dname)

        def ffn(f, epilogue, zhook=None, jhook=None):
            xnT, ring, aT, WoutR, tmpE = T["xnT"], T["ring"], T["aT"], T["WoutR"], T["tmpE"]
            for cg in range(NCG):
                u = _ringuse[0]
                _ringuse[0] += 1
                slot = u % ring_n
                dma("sp", ring[:, slot, :], wbFin[f][cg], [("wbFin", f, cg, 0), ("wbFin", f, cg, 1)], [k("ring", slot)])
                wv = ring[:, slot, :].rearrange("p (dc gu n) -> p dc gu n", dc=8, gu=2)
                for jj in range(2):
                    j = 2 * cg + jj
                    bg = (0, 1)[j % 2]
                    bu = (2, 3)[j % 2]
                    for gu, b in ((0, bg), (1, bu)):
                        for dc in range(8):
                            mm(ps[b][:], wv[:, dc, gu, jj * 128:(jj + 1) * 128], xnT[:, dc, :], dc == 0, dc == 7,
                               [k("ring", slot)] + [k("xnT", s_) for s_ in range(4)], [pk(b)])
                    eb = j % 2
                    E = tmpE[:, eb, :]
                    act(E, ps[bg][:], AF.Exp, [pk(bg)], [k("tmpE", eb)], scale=-1.0)
                    act(E, E, AF.Ln, [k("tmpE", eb), k("cols")], [k("tmpE", eb)], bias=one_c)
                    act(E, E, AF.Exp, [k("tmpE", eb)], [k("tmpE", eb)], scale=-1.0)
                    tt("dve", E, E, ps[bg][:], ALU.mult, [k("tmpE", eb), pk(bg)], [k("tmpE", eb)])
                    tt("dve", aT[:, j, :], E, ps[bu][:], ALU.mult, [k("tmpE", eb), pk(bu)], [k("aT", j)])
                    if jhook is not None:
                        jhook(j)
            pend_e = None
            for s in range(4):
                zb = (4, 5) if s % 2 == 0 else (0, 1)
                for half, b in ((0, zb[0]), (1, zb[1])):
                    for j in range(NJ):
                        mm(ps[b][:], aT[:, j, s * 128:(s + 1) * 128], WoutR[:, j, half * 512:(half + 1) * 512], j == 0, j == NJ - 1,
                           [k("aT", j), k("WoutR", j // 11)], [pk(b)])
                if pend_e is not None:
                    epilogue(*pend_e)
                pend_e = (s, zb)
                if zhook is not None:
                    zhook(s)
            epilogue(*pend_e)

        def post_norm_residual(s, gi, half, zb=(4, 5), xt_ap=None, xkey=None):
            if xt_ap is None:
                xt_ap = T["xt"][:, s, :]
                xkey = k("xt", s)
            tbuf = T["tbuf"]
            i = _ssqi[0] % 8
            _ssqi[0] += 1
            i2 = _ssqi[0] % 8
            _ssqi[0] += 1
            a0, a1 = ssq[:, i:i + 1], ssq[:, i2:i2 + 1]
            rs_c = ssq[:, 8 + i:9 + i]
            act(junkA[:, 0:512], ps[zb[0]][:], AF.Square, [pk(zb[0])], [k("junkA"), k("ssq", i)], accum_out=a0)
            act(junkA[:, 512:1024], ps[zb[1]][:], AF.Square, [pk(zb[1])], [k("junkA"), k("ssq", i2)], accum_out=a1)
            tt("dve", a0, a0, a1, ALU.add, [k("ssq", i), k("ssq", i2)], [k("ssq", i)])
            rstd_from_ssq(rs_c, a0, D, [k("ssq", i)], [k("ssq", 8 + i)], half=half)
            stt("dve", tbuf[:, 0:512], ps[zb[0]][:], rs_c, gpost[:, gi, 0:512], ALU.mult, ALU.mult,
                [pk(zb[0]), k("ssq", 8 + i), k("gpost", gi)], [k("tbuf")])
            stt("dve", tbuf[:, 512:1024], ps[zb[1]][:], rs_c, gpost[:, gi, 512:1024], ALU.mult, ALU.mult,
                [pk(zb[1]), k("ssq", 8 + i), k("gpost", gi)], [k("tbuf")])
            tt("dve", xt_ap, xt_ap, tbuf, ALU.add, [xkey, k("tbuf")], [xkey])

        def alloc_ffn_tiles(tail_mode=False):
            if tail_mode:
                T["xt"] = AR.alloc("xt", [2, D], F32)
                T["xr"] = AR.alloc("xr", [2, D], F32)
            else:
                T["xt"] = AR.alloc("xt", [4, D], F32)
            T["xn"] = AR.alloc("xn", [2, D], BF16)
            T["xnT"] = AR.alloc("xnT", [8, 512], BF16)
            T["aT"] = AR.alloc("aT", [NJ, 512], BF16)
            T["ring"] = AR.alloc("ring", [ring_n, 4096], BF16)
            T["WoutR"] = AR.alloc("WoutR", [NJ, D], BF16)
            T["tmpE"] = AR.alloc("tmpE", [2, 512], F32)
            T["tbuf"] = AR.alloc("tbuf", [D], F32)

        def load_WoutR(f):
            for q in range(2):
                if f == 0:
                    dma("pool", T["WoutR"][:, q * 11:(q + 1) * 11, :],
                        Wf_out[0].rearrange("(j p) n -> p j n", p=128)[:, q * 11:(q + 1) * 11, :], [], [k("WoutR", q)])
                else:
                    dma("sp", T["WoutR"][:, q * 11:(q + 1) * 11, :], wbFout[f][:, q * 11:(q + 1) * 11, :],
                        [("wbFout", f, q, 0), ("wbFout", f, q, 1)], [k("WoutR", q)])

        AR.push()
        alloc_ffn_tiles()
        xt, ring, tmpE = T["xt"], T["ring"], T["tmpE"]
        hT = AR.alloc("hT", [8, 512], BF16)
        cqnT = AR.alloc("cqnT", [3, 512], BF16)
        ckvnT = AR.alloc("ckvnT", [2, 512], BF16)
        sqT = AR.alloc("sqT", [3, 512], BF16)
        rsb = AR.alloc("rsb", [512], F32)
        tM = AR.alloc("tM", [4, 512], F32)
        tW = AR.alloc("tW", [4, 512], F32)
        r1 = AR.alloc("r1", [2, 512], F32)
        r2 = AR.alloc("r2", [2, 512], F32)
        stg = AR.alloc("stg", [4, 512], BF16)
        Vst = AR.alloc("Vst", [4, 2, 65], BF16)
        early_casts_a()
        load_WoutR(0)
        early_casts_b()

        def stage(nparts):
            i = _stgi[0] % 4
            _stgi[0] += 1
            return stg[0:nparts, i, :], k("stg", i)

        def rope_combine(pa, pb, np_, cosT, sinT, tabkey, out, outkey, ra, rb):
            kk_ = _stgi[0] % 2
            tt("dve", r1[0:np_, kk_, :], pa, cosT, ALU.mult, [ra, tabkey], [k("r1", kk_)])
            tt("dve", r2[0:np_, kk_, :], pb, sinT, ALU.mult, [rb, tabkey], [k("r2", kk_)])
            tt("dve", out, r1[0:np_, kk_, :], r2[0:np_, kk_, :], ALU.add, [k("r1", kk_), k("r2", kk_)], [outkey])

        def emit_ag(hf):
            S.add("pool", lambda e: e.collective_compute("AllGather", ALU.bypass, replica_groups=[[0, 1, 2, 3], [4, 5, 6, 7]],
                                                         ins=[lat_s[hf].opt()], outs=[ag[hf].opt()]),
                  [("lat", g_, i_) for g_ in (4 + 2 * hf, 5 + 2 * hf) for i_ in range(2)], [("ag", hf)], dma=True, key=("agk", hf), inc=1)

        G_ORDER = [8, 4, 5, 6, 7, 0, 1, 2, 3]
        for gidx, g in enumerate(G_ORDER):
            sl0 = g * 512
            has_q = g < 8
            if g < 4:
                lat_dst = lat_p[:, g * 512:(g + 1) * 512]
            elif g < 8:
                lat_dst = lat_s[(g - 4) // 2][:, ((g - 4) % 2) * 512:((g - 4) % 2 + 1) * 512]
            else:
                lat_dst = lat_m[:, :]
            def load_x(gg):
                for s in range(4):
                    dma("sp", xt[:, s, :], xin[gg * 512 + s * 128: gg * 512 + (s + 1) * 128, :], [], [k("xt", s)])

            def prenorm():
                norm_transpose_multi(0, T["xnT"], "xnT")

            if gidx == 0:
                load_x(g)
                late_setup()
                prenorm()

            def epi1(s, zb, g=g, sl0=sl0, has_q=has_q):
                if s == 0:
                    dma("sp", tM[0:32], tabM[:, :, sl0:sl0 + 512].rearrange("t p n -> p t n"), [], [k("tM")])
                    dma("sp", tW, tabW[:, :, sl0:sl0 + 512].rearrange("t p n -> p t n"), [], [k("tW")])
                post_norm_residual(s, 0, True, zb)
                if has_q:
                    dma("pool", x1[sl0 + s * 128: sl0 + (s + 1) * 128, :], xt[:, s, :], [k("xt", s)], [("x1", g, s)],
                        key=("x1st", s))
                norm_transpose(s, 1, hT, "hT")

            ffn(0, epi1)

            deferred_casts(g)
            hkeys = [k("hT", s_) for s_ in range(4)]
            loaded = {}

            def need_cg(cgm):
                if cgm in loaded:
                    return loaded[cgm]
                u = _ringuse[0]
                _ringuse[0] += 1
                slot = u % ring_n
                deps = [kk for kk in S.last_write if kk[0] == "wbMix" and kk[1] == cgm]
                dma("sp", ring[:, slot, :], wbMix[cgm].rearrange("p dc n -> p (dc n)"), deps, [k("ring", slot)])
                loaded[cgm] = slot
                return slot

            def proj_chunk(vc, bank, m0=0, m1=128, c0=0, c1=128):
                cgm, ci = vc // 4, vc % 4
                slot = need_cg(cgm)
                wv = ring[:, slot, :].rearrange("p (dc n) -> p dc n", dc=8)
                for dc in range(8):
                    mm(ps[bank][m0:m1, :], wv[:, dc, ci * 128 + c0: ci * 128 + c1], hT[:, dc, :], dc == 0, dc == 7,
                       [k("ring", slot)] + hkeys, [pk(bank)])

            def feat_norm(vcs, banks, n, gcol, gkey, dstT, dname):
                for i, (vc, b) in enumerate(zip(vcs, banks)):
                    proj_chunk(vc, b)
                    act(sqT[:, i, :], ps[b][:], AF.Square, [pk(b)], [k("sqT", i)])
                for i in range(len(vcs)):
                    mm(ps[6][:], ones_bf, sqT[:, i, :], i == 0, i == len(vcs) - 1, [k("ones_bf"), k("sqT", i)], [pk(6)])
                rstd_from_ssq(rsb, ps[6][:], n, [pk(6)], [k("rsb")])
                for i, b in enumerate(banks):
                    stt("dve", dstT[:, i, :], ps[b][:], gcol[:, i:i + 1], rsb, ALU.mult, ALU.mult,
                        [pk(b), k("rsb"), gkey], [k(dname, i)])

            if has_q:
                feat_norm([0, 1, 2], [0, 1, 2], QL, gq, k("gTall"), cqnT, "cqnT")
            feat_norm([3, 4], [3, 0] if has_q else [0, 1], KVL, gkv, k("gTall"), ckvnT, "ckvnT")
            dma("pool", lat_dst[0:256, :].rearrange("(c p) n -> p c n", p=128), ckvnT, [k("ckvnT", 0), k("ckvnT", 1)],
                [("lat", g, 0)], key=("latst", 0))

            ba, bb = 1, 2
            proj_chunk(5, ba, 0, 32, 0, 32)
            proj_chunk(5, bb, 0, 32, 32, 64)
            so, sk_ = stage(32)
            rope_combine(ps[ba][0:32, :], ps[bb][0:32, :], 32, tM[0:32, 2, :], tM[0:32, 3, :], k("tM"), so, sk_, pk(ba), pk(bb))
            dma("pool", lat_dst[256:288, :], so, [sk_], [("lat", g, 1)], key=("latst", 1))

            slot6 = need_cg(1)
            wv6 = ring[:, slot6, :].rearrange("p (dc n) -> p dc n", dc=8)
            psv = ps[3][:].rearrange("p (s n) -> p s n", s=4)
            for s in range(4):
                for dc in range(8):
                    mm(psv[:, s, :], hT[:, dc, s * 128:(s + 1) * 128], wv6[:, dc, 256:384], dc == 0, dc == 7,
                       [k("ring", slot6)] + hkeys, [pk(3)])
            S.add("act", lambda e: e.activation(out=Vst[:, :, :, 0:64], in_=ps[3][:].rearrange("p (s g d) -> p s g d", s=4, g=2),
                                                func=AF.Copy), [pk(3)], [k("Vst")])
            cp("dve", Vst[:, :, :, 64], onescol[:, 4 * g:4 * g + 4].unsqueeze(2).to_broadcast([128, 4, 2]),
               [k("onescol"), k("Vst")], [k("Vst")])
            dma("pool", vW[sl0:sl0 + 512, :].rearrange("(s p) c -> p s c", p=128), Vst.rearrange("p s g e -> p s (g e)"),
                [k("Vst")], [("vW", g)], key=("vWst", 0))

            ba, bb = 0, 1
            proj_chunk(7, ba)
            proj_chunk(8, bb)
            so, sk_ = stage(128)
            rope_combine(ps[ba][:], ps[bb][:], 128, tW[:, 0, :], tW[:, 1, :], k("tW"), so, sk_, pk(ba), pk(bb))
            dma("pool", kW[:, sl0:sl0 + 512], so, [sk_], [("kW", g)], key=("kWst", 0))

            if gidx + 1 < NG and stop_after != ("p1", g):
                load_x(G_ORDER[gidx + 1])
                prenorm()

            if has_q:
                for pr in range(4):
                    ba, bb = ((2, 3), (4, 5), (0, 1))[pr % 3]
                    proj_chunk(9 + 2 * pr, ba)
                    proj_chunk(10 + 2 * pr, bb)
                    so, sk_ = stage(128)
                    rope_combine(ps[ba][:], ps[bb][:], 128, tW[:, 2, :], tW[:, 3, :], k("tW"), so, sk_, pk(ba), pk(bb))
                    dma("pool", qW[pr * 128:(pr + 1) * 128, sl0:sl0 + 512], so, [sk_], [("qW", g, pr)], key=("qWst", pr % 2))
                def gate_chunk(gc):
                    b = (6, 3)[gc % 2]
                    proj_chunk(17 + gc, b)
                    eb = gc % 2
                    E = tmpE[:, eb, :]
                    act(E, ps[b][:], AF.Exp, [pk(b)], [k("tmpE", eb)], scale=-1.0)
                    act(E, E, AF.Ln, [k("tmpE", eb), k("cols")], [k("tmpE", eb)], bias=one_c)
                    so, sk_ = stage(128)
                    act(so, E, AF.Exp, [k("tmpE", eb)], [sk_], scale=-1.0)
                    dma("pool", gT[gc * 128:(gc + 1) * 128, sl0:sl0 + 512], so, [sk_], [("gT", g, gc)], key=("gTst", gc % 2))

                def q_head(h):
                    ba, bb = ((0, 1), (4, 5), (2, 1))[h % 3]
                    for c in range(3):
                        mm(ps[ba][0:96, :], w_uqP[:, c, h, :], cqnT[:, c, :], c == 0, c == 2,
                           [k("w_uqP", c, 0), k("w_uqP", c, 1), k("cqnT", c)], [pk(ba)])
                    for c in range(3):
                        mm(ps[bb][0:32, :], w_uqS[:, c, h, :], cqnT[:, c, :], c == 0, c == 2,
                           [k("w_uqS", c, 0), k("w_uqS", c, 1), k("cqnT", c)], [pk(bb)])
                    so, sk_ = stage(96)
                    act(so[32:64, :], ps[ba][32:64, :], AF.Copy, [pk(ba)], [sk_])
                    act(so[64:96, :], ps[ba][64:96, :], AF.Copy, [pk(ba)], [sk_])
                    rope_combine(ps[ba][0:32, :], ps[bb][0:32, :], 32, tM[0:32, 0, :], tM[0:32, 1, :], k("tM"),
                                 so[0:32, :], sk_, pk(ba), pk(bb))
                    dma("pool", qA[h, :, sl0:sl0 + 512], so, [sk_], [("qA", g, h)], key=("qAst", h % 2))

                for gc in range(16):
                    gate_chunk(gc)
                    trickle_cast(gidx - 1)
                    if gc % 2 == 1:
                        q_head(gc // 2)
            if g == 5:
                emit_ag(0)
            if g == 7:
                emit_ag(1)
            if stop_after == ("p1", g):
                break
        while _defer_q:
            _defer_q.pop(0)()
        AR.pop()

        def finish():
            with nc.allow_low_precision(reason="bf16 matmul operands by design; fp32 accumulation"):
                S.emit()
            nc._sched_stats = (S.stats, S.nwaits, S.n_dma_keys, AR.peak)
            return nc

        if stop_after is not None and stop_after[0] == "p1":
            return finish()

        _epi = {"n": 0, "pend": []}

        def attn_epilogue(bo, out, view4, defer=2, on_dve=False):
            i = _epi["n"] % 2
            _epi["n"] += 1
            recF, recS, numS = T["recF"], T["recS"], T["numS"]
            rF = recF[64:65, i, :]
            if on_dve:
                recip(rF, ps[bo][64:65, :], [pk(bo)], [k("recF", i)])
            else:
                act(rF, ps[bo][64:65, :], AF.Ln, [pk(bo)], [k("recF", i)])
                act(rF, rF, AF.Exp, [k("recF", i)], [k("recF", i)], scale=-1.0)
            cp("dve", recS[64:65, i, 0, :], rF, [k("recF", i)], [k("recS", i, 0)])
            tt("dve", rF, rF, recS[64:65, i, 0, :], ALU.subtract, [k("recF", i), k("recS", i, 0)], [k("recF", i)])
            cp("dve", recS[64:65, i, 1, :], rF, [k("recF", i)], [k("recS", i, 1)])
            if on_dve:
                cp("dve", numS[0:64, i, :], ps[bo][0:64, :], [pk(bo)], [k("numS", i)])
            else:
                act(numS[0:64, i, :], ps[bo][0:64, :], AF.Copy, [pk(bo)], [k("numS", i)])

            def part_b():
                bb_ = (6, 7)[i]
                mm(ps[bb_][0:64, :], ones64[64:65, 0:64], recS[64:65, i, 0, :], True, False, [k("ones64"), k("recS", i, 0)], [pk(bb_)])
                mm(ps[bb_][0:64, :], ones64[64:65, 0:64], recS[64:65, i, 1, :], False, True, [k("ones64"), k("recS", i, 1)], [pk(bb_)])
                if view4:
                    tt("dve", out[0], numS[0:64, i, :].rearrange("p (j n) -> p j n", j=4),
                       ps[bb_][0:64, :].rearrange("p (j n) -> p j n", j=4), ALU.mult, [k("numS", i), pk(bb_)], [out[1]])
                else:
                    tt("dve", out[0], numS[0:64, i, :], ps[bb_][0:64, :], ALU.mult, [k("numS", i), pk(bb_)], [out[1]])

            _epi["pend"].append([defer, part_b])

        def epi_tick(flush=False):
            keep = []
            for item in _epi["pend"]:
                item[0] -= 1
                if flush or item[0] <= 0:
                    item[1]()
                else:
                    keep.append(item)
            _epi["pend"] = keep

        def prep(si):
            AR.push()
            cin = AR.alloc("cin", [2, 2, 512], BF16)
            stgK = AR.alloc("stgK", [2, 4, 512], BF16)
            stgV = AR.alloc("stgV", [2, 8, 4, 65], BF16)
            chunks = []
            if si == 0:
                for i in range(4):
                    chunks.append((lat_p[:, i * 512:(i + 1) * 512], 512, 128 + i * 512, [("lat", i, 0)], 1))
            else:
                for r in range(4):
                    for i in range(4):
                        chunks.append((ag[i // 2][r * 288:(r + 1) * 288, (i % 2) * 512:(i % 2 + 1) * 512], 512,
                                       128 + r * 2048 + i * 512, [("ag", i // 2)], 1))
            chunks.append((lat_m[:, 0:128], 128, 0, [("lat", 8, 0)], 0))
            Kv = Kx[si].rearrange("(pr p) n -> p pr n", p=128)
            for ci, (src, w, k0, rkeys, kcol) in enumerate(chunks):
                buf = ci % 2
                dma("sp", cin[:, buf, :, 0:w], src[0:256, :].rearrange("(c p) n -> p c n", p=128), rkeys, [k("cin", buf)])
                for pr in range(4):
                    b = next_bank((0, 1, 2, 3))
                    for c in range(2):
                        mm(ps[b][:, 0:w], w_uk[:, c, pr * 128:(pr + 1) * 128], cin[:, buf, c, 0:w], c == 0, c == 1,
                           [k("w_uk", c), k("cin", buf)], [pk(b)])
                    act(stgK[:, buf, pr, 0:w], ps[b][:, 0:w], AF.Copy, [pk(b)], [k("stgK", buf, pr)], scale=SC_MLA)
                dma("pool", Kv[:, :, k0:k0 + w], stgK[:, buf, :, 0:w], [k("stgK", buf, pr) for pr in range(4)],
                    [("Kx", si, ci)], key=("Kxst", buf))
                nsb = w // 128
                for s in range(nsb):
                    b = (4, 5)[s % 2]
                    for c in range(2):
                        mm(ps[b][:], cin[:, buf, c, s * 128:(s + 1) * 128], w_uv[:, c, :], c == 0, c == 1,
                           [k("w_uv", c), k("cin", buf)], [pk(b)])
                    cp("dve", stgV[:, buf, :, s, 0:64], ps[b][:].rearrange("p (h d) -> p h d", h=8), [pk(b)], [k("stgV", buf, s)])
                    cp("dve", stgV[:, buf, :, s, 64], onesK[:, kcol:kcol + 1].to_broadcast([128, 8]),
                       [k("onesK"), k("stgV", buf, s)], [k("stgV", buf, s)])
                b0 = k0 // 128
                dma("pool", Vx[si][:, :, b0:b0 + nsb, :].rearrange("h p s e -> p h s e"), stgV[:, buf, :, 0:nsb, :],
                    [k("stgV", buf, s) for s in range(nsb)], [("Vx", si, ci)], key=("Vxst", buf))
            AR.pop()
            return len(chunks)

        def mla(si, nchunks):
            L = LP if si == 0 else LS
            nb = L // 128
            q0 = si * 2048
            oA = T["oA"]
            AR.push()
            KH = AR.alloc("KH", [2, LS], BF16)
            VH = AR.alloc("VH", [2, NBS * 65], BF16)
            QH = AR.alloc("QH", [2, 2048], BF16)
            PT = AR.alloc("PT", [4, 512], BF16)
            T["numS"] = AR.alloc("numS", [2, 512], F32)
            T["recF"] = AR.alloc("recF", [2, 512], F32)
            T["recS"] = AR.alloc("recS", [2, 2, 512], BF16)
            kxkeys = [("Kx", si, ci) for ci in range(nchunks)]
            vxkeys = [("Vx", si, ci) for ci in range(nchunks)]
            LOOK = 3

            def load_head(h):
                kb = h % 2
                dma("sp", KH[32:96, kb, 0:L], Kx[si][h * 64:(h + 1) * 64, 0:L], kxkeys, [k("KH", kb, 0)])
                dma("sp", KH[0:32, kb, 0:128], lat_m[256:288, 0:128], [("lat", 8, 1)], [k("KH", kb, 1)], key=("KHr", kb, 0))
                if si == 0:
                    dma("sp", KH[0:32, kb, 128:L], lat_p[256:288, :], [("lat", g_, 1) for g_ in range(4)], [k("KH", kb, 2)], key=("KHr", kb, 1))
                else:
                    for r in range(4):
                        for hf in range(2):
                            c0_ = 128 + r * 2048 + hf * 1024
                            dma("sp", KH[0:32, kb, c0_: c0_ + 1024], ag[hf][r * 288 + 256:(r + 1) * 288, :],
                                [("ag", hf)], [k("KH", kb, 2 + 2 * r + hf)], key=("KHr", kb, (2 * r + hf) % 4))
                dma("sp", VH[:, kb, 0:nb * 65], Vx[si][h].rearrange("p b e -> p (b e)"), vxkeys, [k("VH", kb)])
                dma("sp", QH[0:96, kb, :], qA[h, :, q0:q0 + 2048], [("qA", g_, h) for g_ in range(si * 4, si * 4 + 4)], [k("QH", kb)])

            nkk = 3 if si == 0 else 10
            units = [(h, qt, b) for h in range(8) for qt in range(4) for b in range(nb)]
            pend = []
            load_head(0)
            for ui, (h, qt, b) in enumerate(units):
                kb = h % 2
                if qt == 0 and b == LOOK + 1 and h + 1 < 8:
                    load_head(h + 1)
                bs = ui % 4
                pt = ui % 4
                kkeys = [k("KH", kb, i_) for i_ in range(nkk)]
                mm(ps[bs][:], KH[0:96, kb, b * 128:(b + 1) * 128], QH[0:96, kb, qt * 512:(qt + 1) * 512], True, True,
                   kkeys + [k("QH", kb)], [pk(bs)])
                act(PT[:, pt, :], ps[bs][:], AF.Exp, [pk(bs)], [k("PT", pt)])
                pend.append((h, qt, b, pt))
                epi_tick()
                if len(pend) > LOOK or ui == len(units) - 1:
                    while pend and (len(pend) > LOOK or ui == len(units) - 1):
                        h2, qt2, b2, pt2 = pend.pop(0)
                        bo = (4, 5)[(h2 * 4 + qt2) % 2]
                        mm(ps[bo][0:65, :], VH[:, h2 % 2, b2 * 65:(b2 + 1) * 65], PT[:, pt2, :], b2 == 0, b2 == nb - 1,
                           [k("VH", h2 % 2), k("PT", pt2)], [pk(bo)])
                        if b2 == nb - 1:
                            attn_epilogue(bo, (oA[0:64, h2, qt2 * 512:(qt2 + 1) * 512], k("oA", h2, qt2)), False,
                                          defer=10, on_dve=True)
            epi_tick(flush=True)
            AR.pop()

        def window(si):
            q0 = si * 2048
            tbase = si * 16
            oB = T["oB"]
            AR.push()
            kWsb = AR.alloc("kWsb", [2, NSLOT], BF16)
            vWsb = AR.alloc("vWsb", [NT, 130], BF16)
            qWsb = AR.alloc("qWsb", [2, 16, 4, 128], BF16)
            PT = AR.alloc("PT", [4, 512], BF16)
            T["numS"] = AR.alloc("numS", [2, 512], F32)
            T["recF"] = AR.alloc("recF", [2, 512], F32)
            T["recS"] = AR.alloc("recS", [2, 2, 512], BF16)
            memset("pool", kWsb[64:128], 0.0, [k("kWsb", "z")])
            memset("pool", qWsb[64:128], 0.0, [k("qWsb", "z")])
            for g in range(2):
                dma("sp", kWsb[0:64, g, :], kW[g * 64:(g + 1) * 64, :], [("kW", g_) for g_ in range(NG)], [k("kWsb", g)])
            dma("sp", vWsb, vW.rearrange("(t p) c -> p t c", p=128), [("vW", g_) for g_ in range(NG)], [k("vWsb")])
            for g in range(2):
                for j in range(4):
                    hh = 4 * g + j
                    dma("sp", qWsb[0:64, g, :, j, :], qW[hh * 64:(hh + 1) * 64, q0:q0 + 2048].rearrange("d (b n) -> d b n", n=128),
                        [("qW", g_, hh // 2) for g_ in range(si * 4, si * 4 + 4)], [k("qWsb", g, j)], key=("qWsbk", j % 2))
            LOOK = 2
            units = []
            for blk in range(16):
                tq = tbase + blk
                prev = tq - 1 if blk > 0 else (None if si == 0 else T_HPREV)
                nxt = tq + 1 if blk < 15 else (None if si == 0 else T_HNEXT)
                kts = []
                if prev is not None:
                    kts.append((prev, 0))
                kts.append((tq, None))
                if nxt is not None:
                    kts.append((nxt, 1))
                kts.append((T_META, None))
                for g in range(2):
                    for i, (kt, mk) in enumerate(kts):
                        units.append((blk, g, i, kt, mk, i == len(kts) - 1))
            pend = []
            for ui, (blk, g, i, kt, mk, last) in enumerate(units):
                bs = ui % 4
                pt = ui % 4
                qap = qWsb[:, g, blk].rearrange("d j n -> d (j n)")
                qkeys = [k("qWsb", g, j) for j in range(4)] + [k("qWsb", "z"), k("kWsb", "z")]
                mm(ps[bs][:], kWsb[:, g, kt * 128:(kt + 1) * 128], qap, True, mk is None, [k("kWsb", g)] + qkeys, [pk(bs)])
                if mk is not None:
                    mm(ps[bs][:], ident, maskPN[:, mk, :], False, True, [k("ident"), k("maskPN")], [pk(bs)])
                act(PT[:, pt, :], ps[bs][:], AF.Exp, [pk(bs)], [k("PT", pt)])
                pend.append((blk, g, i, kt, last, pt))
                epi_tick()
                while pend and (len(pend) > LOOK or ui == len(units) - 1):
                    blk2, g2, i2, kt2, last2, pt2 = pend.pop(0)
                    bo = (4, 5)[(blk2 * 2 + g2) % 2]
                    mm(ps[bo][0:65, :], vWsb[:, kt2, g2 * 65:(g2 + 1) * 65], PT[:, pt2, :], i2 == 0, False,
                       [k("vWsb"), k("PT", pt2)], [pk(bo)])
                    if last2:
                        mm(ps[bo][0:65, :], sel65[0:1, :], sinkrow[0:1, g2, :], False, True, [k("sel65"), k("sinkrow", g2)], [pk(bo)])
                        attn_epilogue(bo, (oB[0:64, 4 * g2:4 * g2 + 4, blk2 * 128:(blk2 + 1) * 128], k("oB", g2, blk2)), True)
            epi_tick(flush=True)
            AR.pop()

        def outproj(si):
            q0 = si * 2048
            oA, oB, mergedT = T["oA"], T["oB"], T["mergedT"]
            AR.push()
            w_oa = AR.alloc("w_oa", [8, D], BF16)
            w_ob = AR.alloc("w_ob", [8, D], BF16)
            gAB = AR.alloc("gAB", [2, 2, 512], BF16)
            t12 = AR.alloc("t12", [2, 2, 512], F32)
            dma("pool", w_oa[0:64], Woa.rearrange("(h p) n -> p h n", p=64), [], [k("w_oa")])
            dma("pool", w_ob[0:64], Wob.rearrange("(h p) n -> p h n", p=64), [], [k("w_ob")])
            memset("pool", w_oa[64:128], 0.0, [k("w_oa", "z")])
            memset("pool", w_ob[64:128], 0.0, [k("w_ob", "z")])
            oAkeys = [k("oA", h, qt) for h in range(8) for qt in range(4)]
            oBkeys = [k("oB", g, blk) for g in range(2) for blk in range(16)]
            for tg in range(4):
                sl = q0 + tg * 512
                gi = si * 4 + tg
                for dmc in range(8):
                    ba, bb = (0, 1) if dmc % 2 == 0 else (2, 3)
                    for h in range(8):
                        mm(ps[ba][:], w_oa[:, h, dmc * 128:(dmc + 1) * 128], oA[:, h, tg * 512:(tg + 1) * 512], h == 0, h == 7,
                           [k("w_oa"), k("w_oa", "z"), k("oA", "z"), k("oA", h, tg)], [pk(ba)])
                    for h in range(8):
                        mm(ps[bb][:], w_ob[:, h, dmc * 128:(dmc + 1) * 128], oB[:, h, tg * 512:(tg + 1) * 512], h == 0, h == 7,
                           [k("w_ob"), k("w_ob", "z"), k("oB", "z")] + [k("oB", h // 4, tg * 4 + b_) for b_ in range(4)], [pk(bb)])
                    gb = dmc % 2
                    dma("sp", gAB[:, 0, gb, :], gT[dmc * 128:(dmc + 1) * 128, sl:sl + 512], [("gT", gi, dmc)], [k("gAB", 0, gb)])
                    dma("sp", gAB[:, 1, gb, :], gT[1024 + dmc * 128:1024 + (dmc + 1) * 128, sl:sl + 512], [("gT", gi, 8 + dmc)],
                        [k("gAB", 1, gb)])
                    tt("dve", t12[:, 0, gb, :], ps[ba][:], gAB[:, 0, gb, :], ALU.mult, [pk(ba), k("gAB", 0, gb)], [k("t12", 0, gb)])
                    tt("dve", t12[:, 1, gb, :], ps[bb][:], gAB[:, 1, gb, :], ALU.mult, [pk(bb), k("gAB", 1, gb)], [k("t12", 1, gb)])
                    tt("dve", mergedT[:, dmc, tg * 512:(tg + 1) * 512], t12[:, 0, gb, :], t12[:, 1, gb, :], ALU.add,
                       [k("t12", 0, gb), k("t12", 1, gb)], [k("mergedT", dmc, tg)])
            if debug:
                dma("pool", oAd[si], oA[0:64].rearrange("p h n -> p (h n)"), oAkeys, [("oAd", si)])
                dma("pool", oBd[si], oB[0:64].rearrange("p h n -> p (h n)"), oBkeys, [("oBd", si)])
                dma("pool", mgd[si], mergedT.rearrange("p c n -> p (c n)"),
                    [k("mergedT", c_, t_) for c_ in range(8) for t_ in range(4)], [("mgd", si)])
            AR.pop()

        def tail(si):
            q0 = si * 2048
            mergedT = T["mergedT"]
            AR.push()
            alloc_ffn_tiles(tail_mode=True)
            xt, xr = T["xt"], T["xr"]
            w_out_sb = AR.alloc("w_out_sb", [8, D], BF16)
            dma("pool", w_out_sb, Wout.rearrange("(c p) n -> p c n", p=128), [], [k("w_out_sb")])
            load_WoutR(1)

            def rows(tg, s):
                sl = q0 + tg * 512 + s * 128
                return sl, sl + 128

            def wout_z(tg, s, zb=(2, 3)):
                r0, r1_ = rows(tg, s)
                dma("sp", xt[:, s % 2, :], x1[r0:r1_, :], [("x1", si * 4 + tg, s)], [k("xt", s % 2)])
                for half, b in ((0, zb[0]), (1, zb[1])):
                    for mc in range(8):
                        mm(ps[b][:], mergedT[:, mc, tg * 512 + s * 128: tg * 512 + (s + 1) * 128],
                           w_out_sb[:, mc, half * 512:(half + 1) * 512], mc == 0, mc == 7,
                           [k("mergedT", mc, tg), k("w_out_sb")], [pk(b)])

            def wout_chain(tg, s, zb=(2, 3)):
                r0, r1_ = rows(tg, s)
                post_norm_residual(s, 1, False, zb, xt_ap=xt[:, s % 2, :], xkey=k("xt", s % 2))
                dma("pool", y[r0:r1_, :], xt[:, s % 2, :], [k("xt", s % 2)], [("y", r0)], key=("yst", s % 2))
                norm_front(s, xi=s % 2)

            def wout_T(tg, s):
                norm_T(s, 7)
                norm_evac(s, 7, 2, T["xnT"], "xnT")

            for s in range(4):
                zb0 = (2, 3) if s % 2 == 0 else (4, 5)
                wout_z(0, s, zb0)
                wout_chain(0, s, zb0)
                if s >= 1:
                    wout_T(0, s - 1)
            wout_T(0, 3)
            for tg in range(4):
                def jhook(j, tg=tg):
                    if tg + 1 < 4 and j == NJ - 3:
                        wout_z(tg + 1, 0, (4, 5))
                        wout_chain(tg + 1, 0, (4, 5))

                def zhook(s, tg=tg):
                    if tg + 1 < 4:
                        if s + 1 < 4:
                            wout_z(tg + 1, s + 1)
                            wout_chain(tg + 1, s + 1)
                        wout_T(tg + 1, s)

                def epi2(s, zb, tg=tg):
                    r0, r1_ = rows(tg, s)
                    dma("sp", xr[:, s % 2, :], y[r0:r1_, :], [("y", r0)], [k("xr", s % 2)])
                    post_norm_residual(s, 2, True, zb, xt_ap=xr[:, s % 2, :], xkey=k("xr", s % 2))
                    dma("pool", y[r0:r1_, :], xr[:, s % 2, :], [k("xr", s % 2)], [("y", r0)], key=("yst2", s % 2))

                ffn(1, epi2, zhook, jhook)
            AR.pop()

        for si in range(2):
            nch = prep(si)
            if stop_after == ("prep", si):
                return finish()
            AR.push()
            T["mergedT"] = AR.alloc("mergedT", [8, 2048], BF16)
            AR.push()
            T["oA"] = AR.alloc("oA", [8, 2048], BF16)
            T["oB"] = AR.alloc("oB", [8, 2048], BF16)
            memset("pool", T["oA"][64:128], 0.0, [k("oA", "z")])
            memset("pool", T["oB"][64:128], 0.0, [k("oB", "z")])
            mla(si, nch)
            window(si)
            outproj(si)
            AR.pop()
            if stop_after == ("attn", si):
                return finish()
            tail(si)
            AR.pop()
        return finish()


def _rope_tables(pos):
    def tab(half, rows_rep, scale):
        inv = 10000.0 ** (-np.arange(half, dtype=np.float64) / half)
        ang = pos[None, :] * inv[:, None]
        cos = np.concatenate([np.cos(ang), np.cos(ang)], 0)
        sin = np.concatenate([-np.sin(ang), np.sin(ang)], 0)
        cos = np.concatenate([cos] * rows_rep, 0)
        sin = np.concatenate([sin] * rows_rep, 0)
        return cos, sin
    cM, sM = tab(16, 1, 1.0)
    cW, sW = tab(32, 2, 1.0)
    tabM = np.stack([cM, sM, cM * SC_MLA, sM * SC_MLA]).astype(np.float32)
    tabW = np.stack([cW, sW, cW * SC_W, sW * SC_W]).astype(np.float32)
    return tabM, tabW


def host_prep(inputs):
    f = lambda k: np.asarray(inputs[k], dtype=np.float32)
    xp, xs, meta = f("x_prompt"), f("x_sample"), f("meta_tokens")
    shared = {
        "ffn1_w_in": f("ffn1_w_in")[0], "ffn2_w_in": f("ffn2_w_in")[0],
        "ffn1_w_out": f("ffn1_w_out")[0], "ffn2_w_out": f("ffn2_w_out")[0],
        "w_in": f("w_in")[0], "w_uq": f("w_uq")[0], "w_ukv": f("w_ukv")[0],
        "w_o_a": f("w_o_a")[0], "w_o_b": f("w_o_b")[0], "w_out": f("w_out")[0],
        "sink": f("sink").reshape(1, 8),
    }
    g_all = np.zeros((8, D), np.float32)
    for i, k in enumerate(("ffn1_pre_g", "ffn1_post_g", "mix_pre_g", "mix_post_g", "ffn2_pre_g", "ffn2_post_g")):
        g_all[i] = f(k)[0]
    g_all[6, :QL] = f("q_norm_g")[0]
    g_all[7, :KVL] = f("kv_norm_g")[0]
    shared["g_all"] = g_all
    shared["ident"] = np.eye(128, dtype=np.float32)
    kk = np.arange(128)[:, None]
    qq = np.arange(128)[None, :]
    mP = np.where(kk >= qq, 0.0, NEGM).astype(np.float32)
    mN = np.where(kk <= qq, 0.0, NEGM).astype(np.float32)
    shared["maskPN"] = np.stack([np.tile(mP, (1, 4)), np.tile(mN, (1, 4))]).astype(np.float32)
    onesK = np.ones((128, 2), np.float32)
    onesK[:112, 0] = 0.0
    shared["onesK"] = onesK
    in_maps = []
    for c in range(NCORES):
        sq, ch = c // 4, c % 4
        xin = np.zeros((NSLOT, D), np.float32)
        xin[0:2048] = xp[c]
        xin[2048:4096] = xs[sq, ch * 2048:(ch + 1) * 2048]
        xin[T_META * 128 + 112: T_META * 128 + 128] = meta
        pos = np.zeros(NSLOT, np.float64)
        pos[0:2048] = 16 + np.arange(2048)
        pos[2048:4096] = 16 + ch * 2048 + np.arange(2048)
        pos[T_META * 128:(T_META + 1) * 128] = np.arange(128) - 112
        onescol = np.zeros((128, NT), np.float32)
        onescol[:, 0:32] = 1.0
        onescol[112:, T_META] = 1.0
        if ch > 0:
            xin[T_HPREV * 128:(T_HPREV + 1) * 128] = xs[sq, ch * 2048 - 128: ch * 2048]
            pos[T_HPREV * 128:(T_HPREV + 1) * 128] = 16 + ch * 2048 - 128 + np.arange(128)
            onescol[:, T_HPREV] = 1.0
        if ch < 3:
            xin[T_HNEXT * 128:(T_HNEXT + 1) * 128] = xs[sq, (ch + 1) * 2048: (ch + 1) * 2048 + 128]
            pos[T_HNEXT * 128:(T_HNEXT + 1) * 128] = 16 + (ch + 1) * 2048 + np.arange(128)
            onescol[:, T_HNEXT] = 1.0
        tabM, tabW = _rope_tables(pos)
        m = dict(shared)
        m.update({"xin": xin, "tabM": tabM, "tabW": tabW, "onescol": onescol})
        in_maps.append(m)
    return in_maps


_NC_CACHE = {}


def kernel(**inputs):
    in_maps = host_prep(inputs)
    if "nc" not in _NC_CACHE:
        _NC_CACHE["nc"] = build_program()
    nc = _NC_CACHE["nc"]
    res = run_bass_kernel_spmd(nc, in_maps, core_ids=list(range(NCORES)))
    y_prompt = np.zeros((8, SEQ_P, D), np.float32)
    y_sample = np.zeros((2, SEQ_S, D), np.float32)
    for c in range(NCORES):
        yc = np.asarray(res.results[c]["y"], dtype=np.float32)
        y_prompt[c] = yc[0:2048]
        y_sample[c // 4, (c % 4) * 2048:(c % 4 + 1) * 2048] = yc[2048:4096]
    return (y_prompt, y_sample)
```
